# Optimizing a Trainium2 kernel written in Bass

```python
import math
import jax, jax.numpy as jnp
from jax import lax
import numpy as np

D_MODEL = 1024
BATCH = 16
SEQ = 4096
DEPTH = 1

CHUNK = 64
N_LEFT_CHUNKS = 8
BAND = N_LEFT_CHUNKS + 1
ATT_HEADS = 8
HEAD_DIM = 64
ATT_WIDTH = ATT_HEADS * HEAD_DIM
REL_CLIP = 128
SSM_WIDTH = D_MODEL // 2
SSM_GROUP = 16
SSM_GROUPS = SSM_WIDTH // SSM_GROUP
SSM_STATE = 64
N_BRANCH = 2
IN_WIDTH = 3 * ATT_WIDTH + SSM_WIDTH + N_BRANCH * D_MODEL
D_FF = 4 * D_MODEL
EPS = 1e-6
DT_MIN = 1e-3
DT_MAX = 1e-1
NEG_INF = -1e30

kernel_name = 'chunk_causal_attn_s5_gated_hybrid'


def rms_norm(x, g):
    xf = x.astype(jnp.float32)
    y = xf * lax.rsqrt(jnp.mean(xf * xf, axis=-1, keepdims=True) + EPS)
    return (y * g.astype(jnp.float32)).astype(x.dtype)


def chunked_band_attention(q, k, v, rel_bias):
    b, s, h, dh = q.shape
    nc = s // CHUNK
    pad = N_LEFT_CHUNKS * CHUNK
    qc = q.reshape(b, nc, CHUNK, h, dh)
    kp = jnp.pad(k, ((0, 0), (pad, 0), (0, 0), (0, 0))).reshape(b, nc + N_LEFT_CHUNKS, CHUNK, h, dh)
    vp = jnp.pad(v, ((0, 0), (pad, 0), (0, 0), (0, 0))).reshape(b, nc + N_LEFT_CHUNKS, CHUNK, h, dh)
    w = jnp.arange(BAND)
    band_idx = jnp.arange(nc)[:, None] + w[None, :]
    kb = kp[:, band_idx]
    vb = vp[:, band_idx]
    scores = jnp.einsum('bnchd,bnwjhd->bnhcwj', qc, kb).astype(jnp.float32) * (dh ** -0.5)
    c = jnp.arange(CHUNK)
    dist = (BAND - 1 - w)[None, :, None] * CHUNK + c[:, None, None] - c[None, None, :]
    bias = rel_bias.astype(jnp.float32)[:, jnp.clip(dist, -REL_CLIP, REL_CLIP) + REL_CLIP]
    valid = (jnp.arange(nc)[:, None] - N_LEFT_CHUNKS + w[None, :]) >= 0
    scores = jnp.where(valid[None, :, None, None, :, None], scores + bias[None, None], NEG_INF)
    probs = jax.nn.softmax(scores.reshape(b, nc, h, CHUNK, BAND * CHUNK), axis=-1)
    probs = probs.reshape(b, nc, h, CHUNK, BAND, CHUNK).astype(v.dtype)
    out = jnp.einsum('bnhcwj,bnwjhd->bnchd', probs, vb)
    return out.reshape(b, s, h * dh)


def s5_ssm(u, a_re, a_im, log_dt, b_re, b_im, c_re, c_im, d_skip):
    f32 = jnp.float32
    b, s, _ = u.shape
    uf = u.astype(f32).reshape(b, s, SSM_GROUPS, SSM_GROUP)
    lam = lax.complex(a_re.astype(f32), a_im.astype(f32))
    dt = jnp.exp(log_dt.astype(f32))[:, None]
    lam_bar = jnp.exp(lam * dt)
    b_bar = ((lam_bar - 1.0) / lam)[..., None] * lax.complex(b_re.astype(f32), b_im.astype(f32))
    bu = jnp.einsum('gph,bsgh->bsgp', b_bar, uf.astype(jnp.complex64))
    a_seq = jnp.broadcast_to(lam_bar, bu.shape)

    def combine(e1, e2):
        a1, x1 = e1
        a2, x2 = e2
        return a1 * a2, a2 * x1 + x2

    _, states = lax.associative_scan(combine, (a_seq, bu), axis=1)
    c_mat = lax.complex(c_re.astype(f32), c_im.astype(f32))
    y = jnp.einsum('ghp,bsgp->bsgh', c_mat, states).real + d_skip.astype(f32) * uf
    return y.reshape(b, s, SSM_WIDTH).astype(u.dtype)


def mixer_block(h, norm_g, w_in, b_gate, rel_bias, a_re, a_im, log_dt, b_re, b_im, c_re, c_im,
                d_skip, w_glu, w_proj_a, w_proj_b, w_out):
    b, s, _ = h.shape
    u = rms_norm(h, norm_g)
    z = u @ w_in
    q, k, v, us, zg = jnp.split(
        z, [ATT_WIDTH, 2 * ATT_WIDTH, 3 * ATT_WIDTH, 3 * ATT_WIDTH + SSM_WIDTH], axis=-1)
    heads = (b, s, ATT_HEADS, HEAD_DIM)
    att = chunked_band_attention(q.reshape(heads), k.reshape(heads), v.reshape(heads), rel_bias)
    y_a = att @ w_proj_a
    ys = jax.nn.gelu(s5_ssm(us, a_re, a_im, log_dt, b_re, b_im, c_re, c_im, d_skip))
    glu_v, glu_g = jnp.split(ys @ w_glu, 2, axis=-1)
    y_b = (glu_v * jax.nn.sigmoid(glu_g)) @ w_proj_b
    gates = jax.nn.sigmoid(zg + b_gate).reshape(b, s, N_BRANCH, D_MODEL)
    mixed = gates[:, :, 0] * y_a + gates[:, :, 1] * y_b
    return mixed @ w_out


def setup_inputs(seed: int = 0) -> dict:
    key = jax.random.key(seed)
    ks = jax.random.split(key, 24)
    L, G, P, Hg = DEPTH, SSM_GROUPS, SSM_STATE, SSM_GROUP
    nrm = lambda k, shape, scale: jax.random.normal(k, shape, jnp.float32) * scale
    x = jax.random.normal(ks[0], (BATCH, SEQ, D_MODEL), jnp.float32)
    norm_mix = 1.0 + nrm(ks[1], (L, D_MODEL), 0.01)
    w_in = nrm(ks[2], (L, D_MODEL, IN_WIDTH), D_MODEL ** -0.5)
    b_gate = nrm(ks[3], (L, N_BRANCH * D_MODEL), 0.01)
    rel_bias = nrm(ks[4], (L, ATT_HEADS, 2 * REL_CLIP + 1), 0.5)
    ssm_a_re = -0.5 * jnp.exp(nrm(ks[5], (L, G, P), 0.01))
    ssm_a_im = math.pi * jnp.arange(P, dtype=jnp.float32)[None, None, :] * (1.0 + nrm(ks[6], (L, G, P), 0.01))
    ssm_log_dt = jax.random.uniform(ks[7], (L, G), jnp.float32, math.log(DT_MIN), math.log(DT_MAX))
    ssm_b_re = nrm(ks[8], (L, G, P, Hg), (2.0 * Hg) ** -0.5)
    ssm_b_im = nrm(ks[9], (L, G, P, Hg), (2.0 * Hg) ** -0.5)
    ssm_c_re = nrm(ks[10], (L, G, Hg, P), (2.0 * P) ** -0.5)
    ssm_c_im = nrm(ks[11], (L, G, Hg, P), (2.0 * P) ** -0.5)
    ssm_d = nrm(ks[12], (L, G, Hg), 1.0)
    w_glu = nrm(ks[13], (L, SSM_WIDTH, 2 * SSM_WIDTH), SSM_WIDTH ** -0.5)
    w_proj_a = nrm(ks[14], (L, ATT_WIDTH, D_MODEL), ATT_WIDTH ** -0.5)
    w_proj_b = nrm(ks[15], (L, SSM_WIDTH, D_MODEL), SSM_WIDTH ** -0.5)
    w_out = nrm(ks[16], (L, D_MODEL, D_MODEL), D_MODEL ** -0.5)
    norm_ffn = 1.0 + nrm(ks[17], (L, D_MODEL), 0.01)
    w_ff1 = nrm(ks[18], (L, D_MODEL, D_FF), D_MODEL ** -0.5)
    w_ff2 = nrm(ks[19], (L, D_FF, D_MODEL), D_FF ** -0.5)
    norm_final = 1.0 + nrm(ks[20], (D_MODEL,), 0.01)
    return {'x': x, 'norm_mix': norm_mix, 'w_in': w_in, 'b_gate': b_gate, 'rel_bias': rel_bias,
            'ssm_a_re': ssm_a_re, 'ssm_a_im': ssm_a_im, 'ssm_log_dt': ssm_log_dt,
            'ssm_b_re': ssm_b_re, 'ssm_b_im': ssm_b_im, 'ssm_c_re': ssm_c_re, 'ssm_c_im': ssm_c_im,
            'ssm_d': ssm_d, 'w_glu': w_glu, 'w_proj_a': w_proj_a, 'w_proj_b': w_proj_b,
            'w_out': w_out, 'norm_ffn': norm_ffn, 'w_ff1': w_ff1, 'w_ff2': w_ff2,
            'norm_final': norm_final}


def reference(x, norm_mix, w_in, b_gate, rel_bias, ssm_a_re, ssm_a_im, ssm_log_dt,
              ssm_b_re, ssm_b_im, ssm_c_re, ssm_c_im, ssm_d, w_glu, w_proj_a, w_proj_b,
              w_out, norm_ffn, w_ff1, w_ff2, norm_final):
    h = x
    for l in range(DEPTH):
        h = h + mixer_block(h, norm_mix[l], w_in[l], b_gate[l], rel_bias[l], ssm_a_re[l], ssm_a_im[l],
                            ssm_log_dt[l], ssm_b_re[l], ssm_b_im[l], ssm_c_re[l], ssm_c_im[l],
                            ssm_d[l], w_glu[l], w_proj_a[l], w_proj_b[l], w_out[l])
        f = rms_norm(h, norm_ffn[l]) @ w_ff1[l]
        h = h + jnp.square(jax.nn.relu(f)) @ w_ff2[l]
    return rms_norm(h, norm_final)
```

```python
import math
from contextlib import ExitStack

import numpy as np
import ml_dtypes
import concourse.bass as bass
import concourse.mybir as mybir
from concourse.bass_utils import run_bass_kernel_spmd

F32 = mybir.dt.float32
BF16 = mybir.dt.bfloat16
I32 = mybir.dt.int32
U8 = mybir.dt.uint8
AF = mybir.ActivationFunctionType
ALU = mybir.AluOpType

NCORES = 8
D = 1024
SEQ = 4096
TOK = 512
NT_SEQ = SEQ // TOK
SEQ_PER_CORE = 2
NTILES = NT_SEQ * SEQ_PER_CORE
EPS = 1e-6
RING = 3
CH_BYTES = 8192
NCH = 39
ESZ = {F32: 4, BF16: 2, I32: 4}
ENGS = ("pe", "act", "dve", "pool", "sp")
GROUPED = ("setup",)
DEBUG = {}


class Op:
    __slots__ = ("eng", "fn", "deps", "sig", "cnt", "dma", "dval", "seq")


class Prog:
    def __init__(self):
        self.ops = {e: [] for e in ENGS}
        self.res = {}
        self.dma_cnt = {}
        self.last_dma = {}

    def _key(self, o):
        return ("d", o.dma) if o.dma is not None else ("e", o.eng)

    def op(self, eng, fn, reads=(), writes=(), dma=None):
        o = Op()
        o.eng, o.fn, o.sig, o.cnt, o.dma, o.dval = eng, fn, False, None, dma, None
        o.seq = len(self.ops[eng])
        cand = {}

        def add(d, kind):
            if d is None:
                return
            k = self._key(d)
            if d.dma is None and d.eng == eng and dma is None:
                if eng == "pe":
                    return
            prev = cand.get(k)
            if prev is None or (d.dval if d.dma is not None else d.seq) > (prev.dval if prev.dma is not None else prev.seq):
                cand[k] = d

        for r in reads:
            st = self.res.get(r)
            if st is not None:
                add(st[0], "raw")
        for w in writes:
            st = self.res.get(w)
            if st is not None:
                add(st[0], "waw")
                for rd in st[1].values():
                    add(rd, "war")
        o.deps = list(cand.values())
        for d in o.deps:
            if d.dma is None:
                d.sig = True
        if dma is not None:
            self.dma_cnt[dma] = self.dma_cnt.get(dma, 0) + 16
            o.dval = self.dma_cnt[dma]
            self.last_dma[dma] = o
        for r in reads:
            st = self.res.get(r)
            if st is None:
                st = self.res[r] = [None, {}]
            st[1][self._key(o)] = o
        for w in writes:
            self.res[w] = [o, {}]
        self.ops[eng].append(o)
        return o

    def barrier(self):
        lasts = []
        for e in ENGS:
            for o in reversed(self.ops[e]):
                if o.dma is None and o.fn is not None:
                    lasts.append(o)
                    break
        dmas = list(self.last_dma.values())
        for e in ENGS:
            o = Op()
            o.eng, o.fn, o.sig, o.cnt, o.dma, o.dval = e, None, False, None, None, None
            o.seq = len(self.ops[e])
            o.deps = [d for d in lasts if d.eng != e] + dmas
            for d in o.deps:
                if d.dma is None:
                    d.sig = True
            self.ops[e].append(o)

    def simulate(self):
        for e in ENGS:
            c = 0
            for o in self.ops[e]:
                if o.sig:
                    c += 1
                    o.cnt = c
        pc = {e: 0 for e in ENGS}
        sem = {("e", e): 0 for e in ENGS}
        for k in self.dma_cnt:
            sem[("d", k)] = 0
        progress = True
        while progress:
            progress = False
            for e in ENGS:
                while pc[e] < len(self.ops[e]):
                    o = self.ops[e][pc[e]]
                    ok = True
                    for d in o.deps:
                        if d.dma is not None:
                            if sem[("d", d.dma)] < (self.dma_cnt[d.dma] if d.dma in GROUPED else d.dval):
                                ok = False
                        elif sem[("e", d.eng)] < d.cnt:
                            ok = False
                    if not ok:
                        break
                    if o.dma is not None:
                        sem[("d", o.dma)] += 16
                    elif o.sig:
                        sem[("e", e)] += 1
                    pc[e] += 1
                    progress = True
        stuck = {e: (pc[e], len(self.ops[e])) for e in ENGS if pc[e] < len(self.ops[e])}
        if stuck:
            msg = []
            for e, (p, n) in stuck.items():
                o = self.ops[e][p]
                msg.append((e, p, n, [(d.eng, d.dma, d.cnt, d.dval, d.seq) for d in o.deps]))
            raise RuntimeError("semaphore deadlock: %s" % msg)

    def emit(self, nc, sems, dma_sems):
        for e in ENGS:
            c = 0
            for o in self.ops[e]:
                if o.sig:
                    c += 1
                    o.cnt = c
        engobj = {"pe": nc.tensor, "act": nc.scalar, "dve": nc.vector, "pool": nc.gpsimd, "sp": nc.sync}

        def run(ename, e):
            waited = {}
            for o in self.ops[ename]:
                for d in o.deps:
                    if d.dma is not None:
                        k, v, s = ("d", d.dma), (self.dma_cnt[d.dma] if d.dma in GROUPED else d.dval), dma_sems[d.dma]
                    else:
                        k, v, s = ("e", d.eng), d.cnt, sems[d.eng]
                    if waited.get(k, 0) < v:
                        e.wait_ge(s, v)
                        waited[k] = v
                if o.fn is None:
                    continue
                inst = o.fn(e)
                if o.dma is not None:
                    inst.then_inc(dma_sems[o.dma], 16)
                elif o.sig:
                    inst.then_inc(sems[ename], 1)
            if ename in ("pool", "sp"):
                for st, tot in self.dma_cnt.items():
                    if waited.get(("d", st), 0) < tot:
                        e.wait_ge(dma_sems[st], tot)
        return run, engobj


def view(pool, off, dtype, shape):
    n = int(np.prod(shape)) * ESZ[dtype]
    v = pool[:, off:off + n].bitcast(dtype)
    if len(shape) == 1:
        return v
    names = "abcdef"[:len(shape)]
    pat = "p (" + " ".join(names) + ") -> p " + " ".join(names)
    return v.rearrange(pat, **{names[k]: int(shape[k]) for k in range(len(shape))})


class Alloc:
    def __init__(self, pool, start=0):
        self.pool, self.off = pool, start

    def __call__(self, dtype, shape, align=64):
        self.off = (self.off + align - 1) // align * align
        n = int(np.prod(shape)) * ESZ[dtype]
        v = view(self.pool, self.off, dtype, shape)
        self.off += n
        return v


CH_WIN = [0, 1, 2, 3]
CH_M, CH_T, CH_ER, CH_EI = 35, 36, 37, 38
CH_GLU = 8
CH_MIX = list(range(9, 17))
CH_WOUT = [17, 18]
CH_W1 = list(range(19, 27))
CH_W2 = list(range(27, 35))


def ffn_plan():
    plan = []
    for step in ("F1_0", "F1_1", "F2_0", "F1_2", "F2_1", "F1_3", "F2_2", "F2_3"):
        kind, r = step.split("_")
        plan.append((kind, int(r)))
    return plan


def tile_chunk_order():
    order = CH_WIN + [CH_M, CH_T, CH_ER, CH_EI, CH_GLU] + CH_MIX + CH_WOUT
    for kind, r in ffn_plan():
        if kind == "F1":
            order += [CH_W1[2 * r], CH_W1[2 * r + 1]]
        else:
            order += [CH_W2[2 * r], CH_W2[2 * r + 1]]
    return order


class _Stop(Exception):
    pass


def build_program(ntiles=NTILES, debug=None, lim=99):
    debug = debug or {}
    nc = bass.Bass("TRN2", target_bir_lowering=False)
    P = Prog()
    ntok = ntiles * TOK

    def din(name, shape, dt=F32):
        return nc.dram_tensor(name, list(shape), dt, kind="ExternalInput").ap()

    x = din("x", [SEQ_PER_CORE * SEQ, D])
    w_in = din("w_in", [D, 4096])
    w_glu = din("w_glu", [512, 1024])
    w_pa = din("w_proj_a", [512, 1024])
    w_pb = din("w_proj_b", [512, 1024])
    w_out = din("w_out", [D, D])
    w_ff1 = din("w_ff1", [D, 4096])
    w_ff2 = din("w_ff2", [4096, D])
    gmix_d = din("gmix_pc", [128, 8])
    gffn_d = din("gffn_pc", [128, 8])
    gfin_d = din("gfin_bc", [128, D])
    bgate_d = din("bgate_pc", [128, 16])
    rbpad_d = din("rbpad", [8, 385])
    rbc_d = din("rbc", [128, 8])
    are_d = din("are_L", [128, 16])
    aim_d = din("aim_L", [128, 16])
    ldt_d = din("ldt_L", [128, 16])
    bre_d = din("bre_L", [128, 16, 16])
    bim_d = din("bim_L", [128, 16, 16])
    cre_d = din("cre_L", [128, 16, 16])
    cim_d = din("cim_L", [128, 16, 16])
    dcol_d = din("dcol", [128, 32])
    identb_d = din("ident_bf", [128, 128], BF16)
    jmat_d = din("jmat_bf", [128, 128], BF16)
    identf_d = din("ident_f", [128, 128])
    tmask_d = din("tmask", [128, 128])
    out = nc.dram_tensor("out", [SEQ_PER_CORE * SEQ, D], F32, kind="ExternalOutput").ap()
    wsc = nc.dram_tensor("wsc", [NCH, 128, CH_BYTES // 2], BF16, kind="Internal").ap()
    dbg_out = {}
    for k, shp in debug.items():
        dbg_out[k] = nc.dram_tensor("dbg_" + k, list(shp), F32, kind="ExternalOutput").ap()

    es = ExitStack()
    with es:
        SB_BYTES = 212736
        pool_t = es.enter_context(nc.sbuf_tensor("sbpool", [128, SB_BYTES], U8))
        pool = pool_t[:]
        PS_t = es.enter_context(nc.psum_tensor("pspool", [128, 4096], F32))
        PS = PS_t[:]
        sems = {e: es.enter_context(nc.semaphore("sem_" + e)) for e in ENGS}
        dma_streams = ["x", "out", "cvl0", "cvl1", "cvs0", "cvs1", "cvs2", "setup", "dbg"] + ["w%d" % s for s in range(RING)]
        dma_sems = {s: es.enter_context(nc.semaphore("dsem_" + s)) for s in dma_streams}

        def psb(b, n=1):
            return PS[:, b * 512:(b + n) * 512]

        def psb16(b, n=1):
            return PS[:, b * 512:(b + n) * 512].bitcast(BF16)

        def PSR(b, n=1):
            return [("ps", b + k) for k in range(n)]

        A = Alloc(pool)
        identb = A(BF16, [128])
        jmat = A(BF16, [128])
        gfin = A(F32, [D])
        bgate = A(F32, [16])
        Hb = A(BF16, [8, 2, 128])
        CT = A(F32, [16, 65])
        ST = A(F32, [16, 65])
        RT = A(F32, [16, 65])
        CR = A(F32, [16])
        CI = A(F32, [16])
        zero1 = A(F32, [1])
        persist_end = A.off

        Xs = [A(F32, [4, D]) for _ in range(2)]
        ust = A(BF16, [4, D])
        sqj = A(BF16, [D])
        ss = A(F32, [4])
        rstd = A(F32, [4])
        inv2 = A(F32, [4])
        uT = A(BF16, [8, TOK])
        aT2 = A(BF16, [8, TOK])
        qT = A(BF16, [4, TOK])
        kT = A(BF16, [4, 2 * TOK])
        Vaug = A(BF16, [8, 8, 128])
        gs = [A(BF16, [TOK]) for _ in range(4)]
        usb = A(BF16, [8, 512])
        Ublk = A(BF16, [32, 64])
        d1r = A(F32, [16, 65])
        d1i = A(F32, [16, 65])
        Wr = A(F32, [16, 65])
        Wi = A(F32, [16, 65])
        Sre = A(BF16, [16, 64])
        Sim = A(BF16, [16, 64])
        ysT = A(BF16, [4, TOK])
        sig = [A(BF16, [TOK]) for _ in range(2)]
        gluT = A(BF16, [4, TOK])
        attT = A(BF16, [4, TOK])
        PT = [A(BF16, [640]) for _ in range(4)]
        rden = [A(F32, [128]) for _ in range(2)]
        t1 = [A(BF16, [TOK]) for _ in range(2)]
        t2 = [A(BF16, [TOK]) for _ in range(2)]
        hid = [A(BF16, [8, TOK]) for _ in range(2)]
        dstg = view(pool, A.off - 16384, F32, [4096])
        ring = [A(BF16, [CH_BYTES // 2]) for _ in range(RING)]
        main_end = A.off
        assert main_end <= SB_BYTES, main_end

        S = Alloc(pool, persist_end)
        identf = S(F32, [128])
        tmask = S(F32, [128])
        gmix = S(F32, [8])
        gffn = S(F32, [8])
        rbc = S(F32, [8])
        dcol = S(F32, [32])
        hstg = S(F32, [16, 128])
        are = S(F32, [16]); aim = S(F32, [16]); ldt = S(F32, [16])
        bre = S(F32, [16, 16]); bim = S(F32, [16, 16]); cre = S(F32, [16, 16]); cim = S(F32, [16, 16])
        sm = {n: S(F32, [16]) for n in ("dt", "adt", "th", "th2", "mag", "rho", "kf", "msk", "sin", "cos",
                                        "lre", "lim", "rm2", "ire", "iim", "lm1", "nre", "nim", "den",
                                        "qre", "qim", "phr", "phi", "p2r", "p2i", "u1", "u2", "u3", "u4")}
        ki = S(I32, [16])
        PPR = S(F32, [16, 9]); PPI = S(F32, [16, 9])
        PNR = S(F32, [16, 8]); PNI = S(F32, [16, 8])
        bbr = S(F32, [16, 16]); bbi = S(F32, [16, 16])
        XR = S(F32, [16, 8, 16]); XI = S(F32, [16, 8, 16])
        MXR = S(F32, [16, 8, 16]); MXI = S(F32, [16, 8, 16])
        YR = S(F32, [16, 9, 16]); YI = S(F32, [16, 9, 16]); YN = S(F32, [16, 9, 16])
        c1 = S(F32, [16, 9, 16]); c2 = S(F32, [16, 9, 16])
        Tmat = S(BF16, [32, 128]); Mmat = S(BF16, [32, 2, 64]); EmR = S(BF16, [32, 128]); EmI = S(BF16, [32, 128])
        tmpT = S(F32, [128])
        stg = [S(F32, [4096]) for _ in range(2)]
        cvb = [S(BF16, [4096]) for _ in range(2)]
        assert S.off <= SB_BYTES, S.off

        def mm(out_, lhsT, rhs, start, stop, reads, writes):
            return P.op("pe", lambda e: e.matmul(out_, lhsT=lhsT, rhs=rhs, start=start, stop=stop), reads, writes)

        def tr(out_, in_, ident, reads, writes):
            return P.op("pe", lambda e: e.transpose(out_, in_, ident), reads, writes)

        def act(out_, in_, func, reads, writes, eng="act", **kw):
            return P.op(eng, lambda e: e.activation(out=out_, in_=in_, func=func, **kw), reads, writes)

        def tt(out_, in0, in1, op, reads, writes, eng="dve"):
            return P.op(eng, lambda e: e.tensor_tensor(out=out_, in0=in0, in1=in1, op=op), reads, writes)

        def tsc(out_, in0, s1, s2, op0, op1, reads, writes, eng="dve"):
            if op1 is None:
                return P.op(eng, lambda e: e.tensor_scalar(out=out_, in0=in0, scalar1=s1, scalar2=None, op0=op0), reads, writes)
            return P.op(eng, lambda e: e.tensor_scalar(out=out_, in0=in0, scalar1=s1, scalar2=s2, op0=op0, op1=op1), reads, writes)

        def stt(out_, in0, scalar, in1, op0, op1, reads, writes):
            return P.op("dve", lambda e: e.scalar_tensor_tensor(out=out_, in0=in0, scalar=scalar, in1=in1, op0=op0, op1=op1), reads, writes)

        def cp(out_, in_, reads, writes, eng="dve"):
            if eng == "act":
                return act(out_, in_, AF.Copy, reads, writes)
            return P.op(eng, lambda e: e.tensor_copy(out=out_, in_=in_), reads, writes)

        def ms(ap, val, writes, eng="dve"):
            return P.op(eng, lambda e: e.memset(ap, val), (), writes)

        def dma(out_, in_, reads, writes, stream, eng="sp"):
            return P.op(eng, lambda e: e.dma_start(out=out_, in_=in_), reads, writes, dma=stream)

        def dbg(name, ap, reads):
            if name in dbg_out:
                dma(dbg_out[name], ap, reads, (), "dbg")

        for (t_, d_, nm) in ((identb, identb_d, "identb"), (jmat, jmat_d, "jmat"), (gfin, gfin_d, "gfin"),
                             (bgate, bgate_d, "bgate"), (identf, identf_d, "identf"), (tmask, tmask_d, "tmask"),
                             (gmix, gmix_d, "gmix"), (gffn, gffn_d, "gffn"), (rbc, rbc_d, "rbc"), (dcol, dcol_d, "dcol"),
                             (are, are_d, "are"), (aim, aim_d, "aim"), (ldt, ldt_d, "ldt"),
                             (bre, bre_d, "bre"), (bim, bim_d, "bim"), (cre, cre_d, "cre"), (cim, cim_d, "cim")):
            dma(t_, d_, (), (nm,), "setup")
        ms(zero1, 0.0, ("zero1",))

        for h in range(8):
            for di, delta in enumerate((0, 128)):
                src = bass.AP(tensor=rbpad_d.tensor, offset=h * 385 + delta + 1, ap=[[1, 128], [1, 128]])
                dma(hstg[:, h * 2 + di, :], src, (), (("hstg", h, di),), "setup")
        for h in range(8):
            tsc(Hb[:, h, :, :], hstg[:, 2 * h:2 * h + 2, :], rbc[:, h:h + 1], None, ALU.subtract, None,
                [("hstg", h, 0), ("hstg", h, 1), "rbc"], [("Hb", h)])
        ms(Hb[0:64, :, 0, 0:64], -30000.0, [("Hb", h) for h in range(8)])

        SR = lambda *n: list(n)

        def smop(kind, o, a, b=None, **kw):
            rd = [a] + ([b] if isinstance(b, str) else [])
            if kind == "tt":
                tt(sm[o], sm[a], sm[b], kw["op"], rd, [o])
            elif kind == "act":
                act(sm[o], sm[a], kw["func"], rd, [o], **{k: v for k, v in kw.items() if k != "func"})
        sm["are"], sm["aim"], sm["ldt"] = are, aim, ldt
        act(sm["dt"], ldt, AF.Exp, ["ldt"], ["dt"])
        tt(sm["adt"], are, sm["dt"], ALU.mult, ["are", "dt"], ["adt"])
        tt(sm["th"], aim, sm["dt"], ALU.mult, ["aim", "dt"], ["th"])
        act(sm["mag"], sm["adt"], AF.Exp, ["adt"], ["mag"])
        act(sm["rho"], sm["adt"], AF.Exp, ["adt"], ["rho"], scale=8.0)
        tsc(sm["th2"], sm["th"], math.pi / 2, None, ALU.add, None, ["th"], ["th2"])
        TWO_PI = 2.0 * math.pi
        for src_, dst_ in (("th", "sin"), ("th2", "cos")):
            tsc(ki, sm[src_], 1.0 / TWO_PI, None, ALU.mult, None, [src_], ["ki"])
            cp(sm["kf"], ki, ["ki"], ["kf"])
            stt(sm["u1"], sm["kf"], -TWO_PI, sm[src_], ALU.mult, ALU.add, ["kf", src_], ["u1"])
            tsc(sm["msk"], sm["u1"], math.pi, None, ALU.is_gt, None, ["u1"], ["msk"])
            stt(sm["u2"], sm["msk"], -TWO_PI, sm["u1"], ALU.mult, ALU.add, ["msk", "u1"], ["u2"])
            tsc(sm["msk"], sm["u2"], -math.pi, None, ALU.is_lt, None, ["u2"], ["msk"])
            stt(sm["u3"], sm["msk"], TWO_PI, sm["u2"], ALU.mult, ALU.add, ["msk", "u2"], ["u3"])
            tsc(sm["u4"], sm["u3"], 3.14159, -3.14159, ALU.min, ALU.max, ["u3"], ["u4"])
            act(sm[dst_], sm["u4"], AF.Sin, ["u4"], [dst_])
        tt(sm["lre"], sm["mag"], sm["cos"], ALU.mult, ["mag", "cos"], ["lre"])
        tt(sm["lim"], sm["mag"], sm["sin"], ALU.mult, ["mag", "sin"], ["lim"])

        def cmul(outr, outi, ar, ai, br, bi, tmp1, tmp2, rd, wr):
            tt(tmp1, ar, br, ALU.mult, rd, ["ctmp1"])
            tt(tmp2, ai, bi, ALU.mult, rd, ["ctmp2"])
            wr_r, wr_i = (wr if isinstance(wr, tuple) else (wr + "_r", wr + "_i"))
            tt(outr, tmp1, tmp2, ALU.subtract, ["ctmp1", "ctmp2"], [wr_r])
            tt(tmp1, ar, bi, ALU.mult, rd, ["ctmp1"])
            tt(tmp2, ai, br, ALU.mult, rd, ["ctmp2"])
            tt(outi, tmp1, tmp2, ALU.add, ["ctmp1", "ctmp2"], [wr_i])

        c1s = c1[:, :, 0, :]
        c2s = c2[:, :, 0, :]
        c1v = c1s[:, :, 0]
        c2v = c2s[:, :, 0]
        ms(PPR[:, :, 0], 1.0, ["PP0_r"])
        ms(PPI[:, :, 0], 0.0, ["PP0_i"])
        cp(PPR[:, :, 1], sm["lre"], ["lre"], ["PP1_r"])
        cp(PPI[:, :, 1], sm["lim"], ["lim"], ["PP1_i"])
        for k in range(2, 9):
            cmul(PPR[:, :, k], PPI[:, :, k], PPR[:, :, k - 1], PPI[:, :, k - 1], sm["lre"], sm["lim"], c1v, c2v,
                 ["PP%d_r" % (k - 1), "PP%d_i" % (k - 1), "lre", "lim"], "PP%d" % k)
        tt(sm["rm2"], sm["mag"], sm["mag"], ALU.mult, ["mag"], ["rm2"])
        P.op("dve", lambda e: e.reciprocal(out=sm["rm2"], in_=sm["rm2"]), ["rm2"], ["rm2"])
        tt(sm["ire"], sm["lre"], sm["rm2"], ALU.mult, ["lre", "rm2"], ["ire"])
        tt(sm["iim"], sm["lim"], sm["rm2"], ALU.mult, ["lim", "rm2"], ["iim_p"])
        tsc(sm["iim"], sm["iim"], -1.0, None, ALU.mult, None, ["iim_p"], ["iim"])
        ms(PNR[:, :, 0], 1.0, ["PN0_r"])
        ms(PNI[:, :, 0], 0.0, ["PN0_i"])
        cp(PNR[:, :, 1], sm["ire"], ["ire"], ["PN1_r"])
        cp(PNI[:, :, 1], sm["iim"], ["iim"], ["PN1_i"])
        for k in range(2, 8):
            cmul(PNR[:, :, k], PNI[:, :, k], PNR[:, :, k - 1], PNI[:, :, k - 1], sm["ire"], sm["iim"], c1v, c2v,
                 ["PN%d_r" % (k - 1), "PN%d_i" % (k - 1), "ire", "iim"], "PN%d" % k)
        tsc(sm["lm1"], sm["lre"], -1.0, None, ALU.add, None, ["lre"], ["lm1"])
        tt(sm["u1"], sm["lm1"], are, ALU.mult, ["lm1", "are"], ["u1"])
        tt(sm["u2"], sm["lim"], aim, ALU.mult, ["lim", "aim"], ["u2"])
        tt(sm["nre"], sm["u1"], sm["u2"], ALU.add, ["u1", "u2"], ["nre"])
        tt(sm["u1"], sm["lim"], are, ALU.mult, ["lim", "are"], ["u1"])
        tt(sm["u2"], sm["lm1"], aim, ALU.mult, ["lm1", "aim"], ["u2"])
        tt(sm["nim"], sm["u1"], sm["u2"], ALU.subtract, ["u1", "u2"], ["nim"])
        tt(sm["u1"], are, are, ALU.mult, ["are"], ["u1"])
        tt(sm["u2"], aim, aim, ALU.mult, ["aim"], ["u2"])
        tt(sm["den"], sm["u1"], sm["u2"], ALU.add, ["u1", "u2"], ["den"])
        P.op("dve", lambda e: e.reciprocal(out=sm["den"], in_=sm["den"]), ["den"], ["den"])
        tt(sm["qre"], sm["nre"], sm["den"], ALU.mult, ["nre", "den"], ["qre"])
        tt(sm["qim"], sm["nim"], sm["den"], ALU.mult, ["nim", "den"], ["qim"])
        qr_b = sm["qre"].unsqueeze(2).broadcast_to([128, 16, 16])
        qi_b = sm["qim"].unsqueeze(2).broadcast_to([128, 16, 16])
        cmul(bbr, bbi, qr_b, qi_b, bre, bim, c1s, c2s, ["qre", "qim", "bre", "bim"], "bb")
        sh4 = [128, 16, 8, 16]
        c1x = c1[:, :, 0:8, :]
        c2x = c2[:, :, 0:8, :]
        pnr_b = PNR.unsqueeze(3).broadcast_to(sh4)
        pni_b = PNI.unsqueeze(3).broadcast_to(sh4)
        bbr_b = bbr.unsqueeze(2).broadcast_to(sh4)
        bbi_b = bbi.unsqueeze(2).broadcast_to(sh4)
        pn_r = ["PN%d_%s" % (k, c) for k in range(8) for c in "ri"]
        cmul(XR, XI, pnr_b, pni_b, bbr_b, bbi_b, c1x, c2x, pn_r + ["bb_r", "bb_i"], "X")
        p7r_b = PPR[:, :, 7:8].unsqueeze(3).broadcast_to(sh4)
        p7i_b = PPI[:, :, 7:8].unsqueeze(3).broadcast_to(sh4)
        cmul(MXR, MXI, p7r_b, p7i_b, XR, XI, c1x, c2x, ["PP7_r", "PP7_i", "X_r", "X_i"], "MX")
        sh9 = [128, 16, 9, 16]
        ppr_b = PPR.unsqueeze(3).broadcast_to(sh9)
        ppi_b = PPI.unsqueeze(3).broadcast_to(sh9)
        cre_b = cre.unsqueeze(2).broadcast_to(sh9)
        cim_b = cim.unsqueeze(2).broadcast_to(sh9)
        pp_r = ["PP%d_%s" % (k, c) for k in range(9) for c in "ri"]
        cmul(YR, YI, ppr_b, ppi_b, cre_b, cim_b, c1, c2, pp_r + ["cre", "cim"], "Y")
        tsc(YN, YI, -1.0, None, ALU.mult, None, ["Y_i"], ["YN"])
        ms(EmR.rearrange("p g n -> p (g n)"), 0.0, ["Emat_r"])
        ms(EmI.rearrange("p g n -> p (g n)"), 0.0, ["Emat_i"])
        for j in range(2):
            pr = slice(64 * j, 64 * j + 64)
            cp(EmR[pr, j::2, :].rearrange("p g (t h) -> p g t h", t=8), YR[pr, :, 1:9, :], ["Y_r", "Emat_r"], ["Emat_r"])
            cp(EmI[pr, j::2, :].rearrange("p g (t h) -> p g t h", t=8), YN[pr, :, 1:9, :], ["YN", "Emat_i"], ["Emat_i"])
        for g in range(32):
            gp, j = g // 2, g % 2
            pr = slice(64 * j, 64 * j + 64)
            bk = g % 4
            o_ = psb(bk)[:, 0:128]
            mm(o_, XR[pr, gp, :, :].rearrange("p t h -> p (t h)"), YR[pr, gp, 0:8, :].rearrange("p t h -> p (t h)"),
               True, False, ["X_r", "Y_r"], PSR(bk))
            mm(o_, XI[pr, gp, :, :].rearrange("p t h -> p (t h)"), YN[pr, gp, 0:8, :].rearrange("p t h -> p (t h)"),
               False, True, ["X_i", "YN"], PSR(bk))
            tt(tmpT, o_, tmask, ALU.mult, PSR(bk) + ["tmask"], ["tmpT"])
            stt(Tmat[:, g, :], identf, dcol[:, g:g + 1], tmpT, ALU.mult, ALU.add, ["identf", "dcol", "tmpT"], [("Tmat", g)])
            bk2 = 4 + g % 4
            o2 = psb(bk2)
            tr(o2[:, 0:64], MXR[pr, gp, :, :].rearrange("p t h -> p (t h)"), identf[pr, 64 * j:64 * j + 64], ["MX_r", "identf"], PSR(bk2))
            tr(o2[:, 64:128], MXI[pr, gp, :, :].rearrange("p t h -> p (t h)"), identf[pr, 64 * j:64 * j + 64], ["MX_i", "identf"], PSR(bk2))
            cp(Mmat[:, g, :, :].rearrange("p a b -> p (a b)"), o2[:, 0:128], PSR(bk2), [("Mmat", g)], eng="act")
        tsc(sm["u1"], sm["rho"], 1.0, None, ALU.mult, None, ["rho"], ["u1"])
        P.op("dve", lambda e: e.reciprocal(out=sm["u1"], in_=sm["u1"]), ["u1"], ["u1"])
        tt(sm["phr"], PPR[:, :, 8], sm["u1"], ALU.mult, ["PP8_r", "u1"], ["phr"])
        tt(sm["phi"], PPI[:, :, 8], sm["u1"], ALU.mult, ["PP8_i", "u1"], ["phi"])
        ms(CT[:, :, 0], 1.0, ["CT"])
        ms(ST[:, :, 0], 0.0, ["ST"])
        cp(CT[:, :, 1], sm["phr"], ["phr", "CT"], ["CT"])
        cp(ST[:, :, 1], sm["phi"], ["phi", "ST"], ["ST"])
        cp(sm["p2r"], sm["phr"], ["phr"], ["p2r"])
        cp(sm["p2i"], sm["phi"], ["phi"], ["p2i"])
        n = 1
        while n < 64:
            cmul(sm["u3"], sm["u4"], sm["p2r"], sm["p2i"], sm["p2r"], sm["p2i"], c1v, c2v, ["p2r", "p2i"], "sq")
            cp(sm["p2r"], sm["u3"], ["sq_r"], ["p2r"])
            cp(sm["p2i"], sm["u4"], ["sq_i"], ["p2i"])
            n *= 2
            hi = min(2 * n, 65)
            w = hi - n
            shw = [128, 16, w]
            cmul(CT[:, :, n:hi], ST[:, :, n:hi], CT[:, :, 0:w], ST[:, :, 0:w],
                 sm["p2r"].unsqueeze(2).broadcast_to(shw), sm["p2i"].unsqueeze(2).broadcast_to(shw),
                 c1.rearrange("p a b c -> p (a b c)")[:, 0:16 * w].rearrange("p (a b) -> p a b", a=16),
                 c2.rearrange("p a b c -> p (a b c)")[:, 0:16 * w].rearrange("p (a b) -> p a b", a=16),
                 ["CT", "ST", "p2r", "p2i"], ("CT", "ST"))
        cp(RT, sm["rho"].unsqueeze(2).broadcast_to([128, 16, 65]), ["rho"], ["RT"])
        ms(RT[:, :, 0], 0.0, ["RT"])

        def kview(w, cols, kc):
            return w.rearrange("(k p) n -> p k n", p=128)[:, :, cols[0]:cols[1]]

        conv = []
        for cb in range(8):
            conv.append((cb, [(0, kview(w_in, (cb * 512, cb * 512 + 512), 8), 8, 512, "gmix")]))
        conv.append((CH_GLU, [(0, kview(w_glu, (0, 1024), 4), 4, 1024, None)]))
        for m in range(8):
            conv.append((CH_MIX[m], [
                (0, kview(w_pa, (m * 128, m * 128 + 128), 4), 4, 128, None),
                (512, kview(w_pb, (m * 128, m * 128 + 128), 4), 4, 128, None),
                (1024, kview(w_in, (2048 + m * 128, 2048 + m * 128 + 128), 8), 8, 128, "gmix"),
                (2048, kview(w_in, (3072 + m * 128, 3072 + m * 128 + 128), 8), 8, 128, "gmix")]))
        for hf in range(2):
            conv.append((CH_WOUT[hf], [(0, kview(w_out, (hf * 512, hf * 512 + 512), 8), 8, 512, None)]))
        for cb in range(8):
            conv.append((CH_W1[cb], [(0, kview(w_ff1, (cb * 512, cb * 512 + 512), 8), 8, 512, "gffn")]))
        for r in range(4):
            for hf in range(2):
                conv.append((CH_W2[2 * r + hf],
                             [(0, kview(w_ff2[r * 1024:(r + 1) * 1024, :], (hf * 512, hf * 512 + 512), 8), 8, 512, None)]))
        for ci, (ch, parts) in enumerate(conv):
            sb, cb_ = stg[ci % 2], cvb[ci % 2]
            sres, cres = ("stg", ci % 2), ("cvb", ci % 2)
            tot = 0
            for (off, src, kc, ncol, sc) in parts:
                dv = sb[:, off:off + kc * ncol].rearrange("p (k n) -> p k n", k=kc)
                dma(dv, src, (), [sres], "cvl%d" % (ci % 2))
                tot = max(tot, off + kc * ncol)
            for pi, (off, src, kc, ncol, sc) in enumerate(parts):
                sv = sb[:, off:off + kc * ncol].rearrange("p (k n) -> p k n", k=kc)
                ov = cb_[:, off:off + kc * ncol].rearrange("p (k n) -> p k n", k=kc)
                if sc is None:
                    cp(ov, sv, [sres], [cres], eng=("act" if ci % 2 == 0 else "dve"))
                else:
                    gv = (gmix if sc == "gmix" else gffn).unsqueeze(2).broadcast_to([128, kc, ncol])
                    tt(ov, sv, gv, ALU.mult, [sres, sc], [cres], eng=("dve" if ci % 3 else "pool"))
            dma(wsc[ch, :, 0:tot], cb_[:, 0:tot], [cres], [("wsc", ch)], "cvs%d" % (ci % 2))
        dma(wsc[CH_M], Mmat.rearrange("p a b c -> p (a b c)"), [("Mmat", g) for g in range(32)], [("wsc", CH_M)], "cvs2")
        dma(wsc[CH_T], Tmat.rearrange("p a b -> p (a b)"), [("Tmat", g) for g in range(32)], [("wsc", CH_T)], "cvs2")
        dma(wsc[CH_ER], EmR.rearrange("p a b -> p (a b)"), ["Emat_r"], [("wsc", CH_ER)], "cvs2")
        dma(wsc[CH_EI], EmI.rearrange("p a b -> p (a b)"), ["Emat_i"], [("wsc", CH_EI)], "cvs2")
        if "Tmat" in dbg_out:
            cp(stg[0], Tmat.rearrange("p a b -> p (a b)"), [("Tmat", g) for g in range(32)], [("stg", 0)])
            dbg("Tmat", stg[0], [("stg", 0)])
        if "Mmat" in dbg_out:
            cp(stg[1], Mmat.rearrange("p a b c -> p (a b c)"), [("Mmat", g) for g in range(32)], [("stg", 1)])
            dbg("Mmat", stg[1], [("stg", 1)])
        if "tabs" in dbg_out:
            dbg("tabs", CT.rearrange("p a b -> p (a b)"), ["CT"])

        P.barrier()

        ms(Vaug[:, :, :, 64:128], 1.0, ["Vones"], eng="pool")

        order = tile_chunk_order()
        nper = len(order)
        wstate = {"n": 0, "loaded": 0}
        total_chunks = ntiles * nper

        def wnext(expect, keep=0):
            n = wstate["n"]
            assert order[n % nper] == expect, (order[n % nper], expect)
            while wstate["loaded"] < min(n - keep + RING, total_chunks):
                k = wstate["loaded"]
                s_ = k % RING
                ch_ = order[k % nper]
                L_ = 3072 if ch_ in CH_MIX else CH_BYTES // 2
                dma(ring[s_][:, 0:L_], wsc[ch_, :, 0:L_], [("wsc", ch_)], [("ring", s_)], "w%d" % s_)
                wstate["loaded"] += 1
            wstate["n"] += 1
            return ring[n % RING], ("ring", n % RING)

        bank_rr = {"i": 0}

        def nbank(lst=(0, 1, 2, 3, 4, 5, 6, 7)):
            b_ = lst[bank_rr["i"] % len(lst)]
            bank_rr["i"] += 1
            return b_

        def XR_(xb, ts=None):
            return [("X", xb, t) for t in range(4)] if ts is None else [("X", xb, ts)]

        def rms_stats(X, xb, tss=(0, 1, 2, 3)):
            for ts in tss:
                act(sqj, X[:, ts, :], AF.Square, XR_(xb, ts), ["sqj", ("ss", ts)], accum_out=ss[:, ts:ts + 1])
                r_ = rstd[:, ts:ts + 1]
                tsc(r_, ss[:, ts:ts + 1], 1.0 / D, EPS, ALU.mult, ALU.add, [("ss", ts)], [("rstd", ts)])
                act(r_, r_, AF.Sqrt, [("rstd", ts)], [("rstd", ts)])
                P.op("dve", lambda e, r_=r_: e.reciprocal(out=r_, in_=r_), [("rstd", ts)], [("rstd", ts)])

        def norm_scale(X, xb, ts):
            rms_stats(X, xb, (ts,))
            act(ust[:, ts, :], X[:, ts, :], AF.Copy, XR_(xb, ts) + [("rstd", ts)], [("ust", ts)], scale=rstd[:, ts:ts + 1])

        def norm_tr(ts, dst, dst_res, banks):
            b_ = nbank(banks)
            pv = psb16(b_)
            for kc in range(8):
                tr(pv[:, kc * 128:(kc + 1) * 128], ust[:, ts, kc * 128:(kc + 1) * 128], identb, [("ust", ts)], PSR(b_))
            cp(dst[:, :, ts * 128:(ts + 1) * 128], pv.rearrange("p (k t) -> p k t", k=8), PSR(b_), [dst_res(ts)],
               eng=("act" if ts % 2 else "dve"))

        def rmsnorm_to_T(X, xb, dst, dst_res, banks):
            for ts in range(4):
                norm_scale(X, xb, ts)
                norm_tr(ts, dst, dst_res, banks)

        def pre(i):
            xb = i % 2
            X = Xs[xb]
            xrows = x[i * TOK:(i + 1) * TOK, :].rearrange("(ts p) d -> p ts d", p=128)
            dma(X, xrows, (), XR_(xb), "x")
            rmsnorm_to_T(X, xb, uT, lambda ts: "uT", (0, 1))

        def body(i):
            ti = i % NT_SEQ
            par = ti % 2
            xb = i % 2
            X = Xs[xb]
            wq, rq = wnext(0)
            wqv = wq.rearrange("p (k n) -> p k n", k=8)
            for m in range(4):
                b = nbank((2, 3, 4, 5))
                for kc in range(8):
                    mm(psb(b), wqv[:, kc, m * 128:(m + 1) * 128], uT[:, kc, :], kc == 0, kc == 7, [rq, "uT"], PSR(b))
                act(qT[:, m, :], psb(b), AF.Copy, PSR(b), [("qT", m)], scale=0.125)
            wk, rk = wnext(1)
            wkv = wk.rearrange("p (k n) -> p k n", k=8)
            for m in range(4):
                b = nbank((2, 3, 4, 5))
                for kc in range(8):
                    mm(psb(b), wkv[:, kc, m * 128:(m + 1) * 128], uT[:, kc, :], kc == 0, kc == 7, [rk, "uT"], PSR(b))
                cp(kT[:, m, par * TOK:(par + 1) * TOK], psb(b), PSR(b), [("kT", par, m)], eng=("dve" if m % 2 else "act"))
            wv, rv = wnext(2)
            wvv = wv.rearrange("p (k n) -> p k n", k=8)
            for ts in range(4):
                b = nbank((2, 3, 4, 5))
                for kc in range(8):
                    mm(psb(b), uT[:, kc, ts * 128:(ts + 1) * 128], wvv[:, kc, :], kc == 0, kc == 7, [rv, "uT"], PSR(b))
                slot = par * 4 + ts
                pv8 = psb(b).rearrange("p (h d) -> p h d", h=8)
                cp(Vaug[:, slot, :, 0:64], pv8, PSR(b), [("V", slot)], eng="dve")
            wu, ru = wnext(3)
            wuv = wu.rearrange("p (k n) -> p k n", k=8)
            usbg = usb.rearrange("p t n -> p (t n)").rearrange("p (g t h) -> p g t h", g=32, t=8)
            for a in range(4):
                b = nbank((2, 3, 4, 5))
                for tl in range(2):
                    for kc in range(8):
                        lh = uT[:, kc, 2 * a + tl::8]
                        mm(psb(b)[64 * tl:64 * tl + 64, :], lh, wuv[:, kc, :], kc == 0, kc == 7, [ru, "uT"], PSR(b))
                ev = "act" if a % 2 == 0 else "dve"
                cp(usbg[0:64, :, 2 * a, :], psb(b)[0:64, :].rearrange("c (g h) -> c g h", g=32), PSR(b), [("usb", 2 * a)], eng=ev)
                cp(usbg[0:64, :, 2 * a + 1, :], psb(b)[64:128, :].rearrange("c (g h) -> c g h", g=32), PSR(b), [("usb", 2 * a + 1)], eng=ev)

            wM, rM = wnext(CH_M)
            Mv = wM.rearrange("p (g a b) -> p g a b", g=32, a=2)
            usbres = [("usb", t) for t in range(8)]
            pU = psb16(6, 2)
            for g in range(32):
                tr(pU[:, g * 64:(g + 1) * 64], usb.rearrange("p t n -> p (t n)")[0:64, g * 128:(g + 1) * 128], identb[0:64, 0:64],
                   usbres, PSR(6 + g // 16))
            cp(Ublk[:, 0:16, :].rearrange("p a b -> p (a b)"), pU[:, 0:1024], PSR(6), ["Ublk0"], eng="act")
            cp(Ublk[:, 16:32, :].rearrange("p a b -> p (a b)"), pU[:, 1024:2048], PSR(7), ["Ublk1"], eng="dve")
            pR = psb(2, 2).rearrange("p (g c) -> p g c", g=16)
            pI = psb(4, 2).rearrange("p (g c) -> p g c", g=16)
            for g in range(32):
                gp, j = g // 2, g % 2
                pr = slice(64 * j, 64 * j + 64)
                ub = "Ublk%d" % (g // 16)
                mm(pR[pr, gp, :], Mv[:, g, 0, :], Ublk[:, g, :], True, True, [rM, ub], PSR(2 + gp // 8))
                mm(pI[pr, gp, :], Mv[:, g, 1, :], Ublk[:, g, :], True, True, [rM, ub], PSR(4 + gp // 8))
            if ti == 0:
                ms(CR, 0.0, ["CR"])
                ms(CI, 0.0, ["CI"])
            RR, RI_ = PSR(2, 2), PSR(4, 2)
            tt(Wr[:, :, 0:64], pR, CT[:, :, 1:65], ALU.mult, RR + ["CT"], ["Wr"])
            tt(Wi[:, :, 0:64], pI, ST[:, :, 1:65], ALU.mult, RI_ + ["ST"], ["Wi"])
            tt(d1r[:, :, 1:65], Wr[:, :, 0:64], Wi[:, :, 0:64], ALU.add, ["Wr", "Wi"], ["d1r"])
            tt(Wr[:, :, 0:64], pI, CT[:, :, 1:65], ALU.mult, RI_ + ["CT"], ["Wr"])
            tt(Wi[:, :, 0:64], pR, ST[:, :, 1:65], ALU.mult, RR + ["ST"], ["Wi"])
            tt(d1i[:, :, 1:65], Wr[:, :, 0:64], Wi[:, :, 0:64], ALU.subtract, ["Wr", "Wi"], ["d1i"])
            cp(d1r[:, :, 0], CR, ["CR", "d1r"], ["d1r"])
            cp(d1i[:, :, 0], CI, ["CI", "d1i"], ["d1i"])
            def scan_part():
                fl = lambda a_: a_.rearrange("p a b -> p (a b)")
                P.op("dve", lambda e: e.tensor_tensor_scan(out=fl(Wr), data0=fl(RT), data1=fl(d1r), initial=0.0, op0=ALU.mult, op1=ALU.add),
                     ["RT", "d1r"], ["Wr"])
                P.op("dve", lambda e: e.tensor_tensor_scan(out=fl(Wi), data0=fl(RT), data1=fl(d1i), initial=0.0, op0=ALU.mult, op1=ALU.add),
                     ["RT", "d1i"], ["Wi"])
            def scan_part2():
                tA, tB = d1r, d1i
                tt(tA, Wr, CT, ALU.mult, ["Wr", "CT"], ["d1r"])
                tt(tB, Wi, ST, ALU.mult, ["Wi", "ST"], ["d1i"])
                tt(Sre, tA[:, :, 0:64], tB[:, :, 0:64], ALU.subtract, ["d1r", "d1i"], ["Sre"])
                tt(CR, tA[:, :, 64], tB[:, :, 64], ALU.subtract, ["d1r", "d1i"], ["CR"])
            def scan_part3():
                tA, tB = d1r, d1i
                tt(tA, Wr, ST, ALU.mult, ["Wr", "ST"], ["d1r"])
                tt(tB, Wi, CT, ALU.mult, ["Wi", "CT"], ["d1i"])
                tt(Sim, tA[:, :, 0:64], tB[:, :, 0:64], ALU.add, ["d1r", "d1i"], ["Sim"])
                tt(CI, tA[:, :, 64], tB[:, :, 64], ALU.add, ["d1r", "d1i"], ["CI"])


            items = [(qt, h) for qt in range(4) for h in range(8)]

            def att_S(k):
                qt, h = items[k]
                J = ti * 4 + qt
                kts = [kt for kt in range(J - 4, J + 1) if kt >= 0]
                hp = slice(64 * (h % 2), 64 * (h % 2) + 64)
                chh = h // 2
                sb_ = 0 if k % 2 == 0 else 2
                pt, ptr = PT[k % 4], ("PT", k % 4)
                sreg = psb(sb_, 2)
                for kt in kts:
                    r = kt - (J - 4)
                    kpar = (kt // 4) % 2
                    kcol = kpar * TOK + (kt % 4) * 128
                    bankr = sb_ + (r * 128) // 512
                    o_ = sreg[:, r * 128:(r + 1) * 128]
                    near = r >= 3
                    mm(o_, kT[hp, chh, kcol:kcol + 128], qT[hp, chh, qt * 128:(qt + 1) * 128], True, not near,
                       [("kT", kpar, chh), ("qT", chh)], PSR(bankr))
                    if near:
                        mm(o_, jmat, Hb[:, h, 4 - r, :], False, True, ["jmat", ("Hb", h)], PSR(bankr))
                r0 = kts[0] - (J - 4)
                act(pt[:, r0 * 128:640], sreg[:, r0 * 128:640], AF.Exp, PSR(sb_, 2), [ptr])
                if r0 == 0:
                    ms(pt[0:64, 64:128], 0.0, [ptr], eng="pool")

            def att_PV(k):
                qt, h = items[k]
                J = ti * 4 + qt
                kts = [kt for kt in range(J - 4, J + 1) if kt >= 0]
                chh = h // 2
                pt, ptr = PT[k % 4], ("PT", k % 4)
                accb = 4 if qt % 2 == 0 else 6
                ao = psb(accb, 2)[:, h * 128:(h + 1) * 128]
                ab = accb + h // 4
                seq_k = [J] + [kt for kt in kts if kt != J]
                for idx, kt in enumerate(seq_k):
                    r = kt - (J - 4)
                    slot = ((kt // 4) % 2) * 4 + kt % 4
                    last = idx == len(seq_k) - 1
                    vw = Vaug[:, slot, :, :].rearrange("p h d -> p (h d)")
                    c0 = h * 128 - 64 * (h % 2)
                    mm(ao, vw[:, c0:c0 + 128], pt[:, r * 128:(r + 1) * 128], idx == 0, last, [("V", slot), "Vones", ptr], PSR(ab))
                rd, rdr = rden[h % 2], ("rden", h % 2)
                lo, hi_ = (slice(0, 64), slice(64, 128)) if h % 2 == 0 else (slice(64, 128), slice(0, 64))
                P.op("dve", lambda e: e.reciprocal(out=rd[lo, :], in_=ao[hi_, :]), PSR(ab), [rdr])
                tt(attT[lo, chh, qt * 128:(qt + 1) * 128], ao[lo, :], rd[lo, :], ALU.mult, PSR(ab) + [rdr], [("attT", chh)])

            att_S(0)
            for k in range(len(items)):
                if k + 1 < len(items):
                    att_S(k + 1)
                att_PV(k)
                if k == 7:
                    scan_part()
                elif k == 15:
                    scan_part2()
                elif k == 23:
                    scan_part3()

            wT, rT = wnext(CH_T)
            Tv = wT.rearrange("p (g n) -> p g n", g=32)
            wER, rER = wnext(CH_ER, keep=1)
            ERv = wER.rearrange("p (g n) -> p g n", g=32)
            wEI, rEI = wnext(CH_EI, keep=2)
            EIv = wEI.rearrange("p (g n) -> p g n", g=32)
            for qd in range(4):
                yb = 0 if qd % 2 == 0 else 2
                pY = psb(yb, 2)
                for gl in range(8):
                    g = qd * 8 + gl
                    gp = g // 2
                    o_ = pY[0:64, gl * 128:(gl + 1) * 128]
                    bb_ = PSR(yb + gl // 4)
                    mm(o_, Ublk[:, g, :], Tv[:, g, :], True, False, [rT, "Ublk%d" % (g // 16)], bb_)
                    mm(o_, Sre[:, gp, :], ERv[:, g, :], False, False, [rER, "Sre"], bb_)
                    mm(o_, Sim[:, gp, :], EIv[:, g, :], False, True, [rEI, "Sim"], bb_)
                act(usb[0:64, :, qd * 128:(qd + 1) * 128].rearrange("c t (g h) -> c t g h", g=8),
                    pY[0:64, :].rearrange("c (g t h) -> c t g h", g=8, t=8), AF.Gelu_apprx_tanh,
                    PSR(yb, 2), [("usb", t) for t in range(8)])
            pZ = psb16(4, 2)
            for fc in range(4):
                for t in range(8):
                    col = (fc * 8 + t) * 64
                    tr(pZ[:, col:col + 64], usb[0:64, t, fc * 128:(fc + 1) * 128], identb[0:64, 0:64], usbres, PSR(4 + col // 1024))
            for half in range(2):
                cp(ysT[:, 2 * half:2 * half + 2, :].rearrange("p f (c t) -> p f t c", t=8),
                   pZ[:, half * 1024:(half + 1) * 1024].rearrange("p (f t c) -> p f t c", f=2, t=8), PSR(4 + half),
                   [("ysT", half)], eng=("act" if half else "dve"))

            wg, rg = wnext(CH_GLU)
            wgv = wg.rearrange("p (k n) -> p k n", k=4)
            for m in range(4):
                b1 = nbank()
                for kc in range(4):
                    mm(psb(b1), wgv[:, kc, 512 + m * 128:512 + (m + 1) * 128], ysT[:, kc, :], kc == 0, kc == 3, [rg, ("ysT", kc // 2)], PSR(b1))
                act(sig[m % 2], psb(b1), AF.Sigmoid, PSR(b1), [("sig", m % 2)])
                b2 = nbank()
                for kc in range(4):
                    mm(psb(b2), wgv[:, kc, m * 128:(m + 1) * 128], ysT[:, kc, :], kc == 0, kc == 3, [rg, ("ysT", kc // 2)], PSR(b2))
                tt(gluT[:, m, :], psb(b2), sig[m % 2], ALU.mult, PSR(b2) + [("sig", m % 2)], [("gluT", m)])

            for m in range(8):
                wm, rm = wnext(CH_MIX[m])
                wa = wm[:, 0:512].rearrange("p (k n) -> p k n", k=4)
                wb = wm[:, 512:1024].rearrange("p (k n) -> p k n", k=4)
                wga = wm[:, 1024:2048].rearrange("p (k n) -> p k n", k=8)
                wgb = wm[:, 2048:3072].rearrange("p (k n) -> p k n", k=8)
                bga, bgb, bya, byb = nbank(), nbank(), nbank(), nbank()
                for kc in range(8):
                    mm(psb(bga), wga[:, kc, :], uT[:, kc, :], kc == 0, kc == 7, [rm, "uT"], PSR(bga))
                ga = gs[(2 * m) % 4]
                act(ga, psb(bga), AF.Sigmoid, PSR(bga), [("gs", (2 * m) % 4)], bias=bgate[:, m:m + 1])
                for kc in range(8):
                    mm(psb(bgb), wgb[:, kc, :], uT[:, kc, :], kc == 0, kc == 7, [rm, "uT"], PSR(bgb))
                gb = gs[(2 * m + 1) % 4]
                act(gb, psb(bgb), AF.Sigmoid, PSR(bgb), [("gs", (2 * m + 1) % 4)], bias=bgate[:, 8 + m:9 + m])
                for kc in range(4):
                    mm(psb(bya), wa[:, kc, :], attT[:, kc, :], kc == 0, kc == 3, [rm, ("attT", kc)], PSR(bya))
                for kc in range(4):
                    mm(psb(byb), wb[:, kc, :], gluT[:, kc, :], kc == 0, kc == 3, [rm, ("gluT", kc)], PSR(byb))
                tt(t1[m % 2], psb(bya), ga, ALU.mult, PSR(bya) + [("gs", (2 * m) % 4)], [("t1", m % 2)])
                tt(t2[m % 2], psb(byb), gb, ALU.mult, PSR(byb) + [("gs", (2 * m + 1) % 4)], [("t2", m % 2)])
                tt(aT2[:, m, :], t1[m % 2], t2[m % 2], ALU.add, [("t1", m % 2), ("t2", m % 2)], [("aT2", t_) for t_ in range(4)], eng="pool")

            wo0, ro0 = wnext(CH_WOUT[0])
            wo1, ro1 = wnext(CH_WOUT[1], keep=1)
            wovs = [(wo0.rearrange("p (k n) -> p k n", k=8), ro0), (wo1.rearrange("p (k n) -> p k n", k=8), ro1)]
            for ts in range(4):
                for hf in range(2):
                    wov, ro = wovs[hf]
                    b = nbank()
                    for kc in range(8):
                        mm(psb(b), aT2[:, kc, ts * 128:(ts + 1) * 128], wov[:, kc, :], kc == 0, kc == 7, [ro, ("aT2", ts)], PSR(b))
                    xs = X[:, ts, hf * 512:(hf + 1) * 512]
                    tt(xs, psb(b), xs, ALU.add, PSR(b) + XR_(xb, ts), XR_(xb, ts))
                act(ust[:, ts, :], X[:, ts, :], AF.Copy, XR_(xb, ts), [("ust", ts)])
                act(sqj, X[:, ts, :], AF.Square, XR_(xb, ts), ["sqj", ("ss", ts)], accum_out=ss[:, ts:ts + 1])
                tsc(inv2[:, ts:ts + 1], ss[:, ts:ts + 1], 1.0 / D, EPS, ALU.mult, ALU.add, [("ss", ts)], [("inv2", ts)])
                P.op("dve", lambda e, ts=ts: e.reciprocal(out=inv2[:, ts:ts + 1], in_=inv2[:, ts:ts + 1]), [("inv2", ts)], [("inv2", ts)])
                if ts >= 1:
                    norm_tr(ts - 1, aT2, lambda t_: ("aT2", t_), (0, 1))
            norm_tr(3, aT2, lambda t_: ("aT2", t_), (0, 1))

        def ffn_post(i):
            xb = i % 2
            X = Xs[xb]
            for step, (kind, r) in enumerate(ffn_plan()):
                hb = hid[r % 2]
                if kind == "F1":
                    for c2_ in range(2):
                        w1, r1 = wnext(CH_W1[2 * r + c2_])
                        w1v = w1.rearrange("p (k n) -> p k n", k=8)
                        for mq in range(4):
                            mmi = c2_ * 4 + mq
                            b = nbank((0, 1, 2, 3))
                            for kc in range(8):
                                mm(psb(b), w1v[:, kc, mq * 128:(mq + 1) * 128], aT2[:, kc, :], kc == 0, kc == 7, [r1] + [("aT2", t_) for t_ in range(4)], PSR(b))
                            hres = ("hid", r % 2, mmi)
                            if mmi % 2 == 0:
                                act(hb[:, mmi, :], psb(b), AF.Relu, PSR(b), [hres])
                            else:
                                tsc(hb[:, mmi, :], psb(b), 0.0, None, ALU.max, None, PSR(b), [hres])
                            tt(hb[:, mmi, :], hb[:, mmi, :], hb[:, mmi, :], ALU.mult, [hres], [hres], eng="pool")
                else:
                    for hf in range(2):
                        w2, r2 = wnext(CH_W2[2 * r + hf])
                        w2v = w2.rearrange("p (k n) -> p k n", k=8)
                        for ts in range(4):
                            b = nbank((4, 5, 6, 7))
                            for ks in range(8):
                                mm(psb(b), hb[:, ks, ts * 128:(ts + 1) * 128], w2v[:, ks, :], ks == 0, ks == 7,
                                   [r2, ("hid", r % 2, ks)], PSR(b))
                            xs = X[:, ts, hf * 512:(hf + 1) * 512]
                            stt(xs, psb(b), inv2[:, ts:ts + 1], xs, ALU.mult, ALU.add, PSR(b) + XR_(xb, ts) + [("inv2", ts)], XR_(xb, ts))
                if step == 3 and i + 1 < ntiles:
                    pre(i + 1)
            rms_stats(X, xb)
            for ts in range(4):
                stt(X[:, ts, :], X[:, ts, :], rstd[:, ts:ts + 1], gfin, ALU.mult, ALU.mult, XR_(xb, ts) + [("rstd", ts), "gfin"], XR_(xb, ts))
            orows = out[i * TOK:(i + 1) * TOK, :].rearrange("(ts p) d -> p ts d", p=128)
            dma(orows, X, XR_(xb), (), "out")

        if ntiles > 0:
            pre(0)
        for i in range(ntiles):
            body(i)
            ffn_post(i)

        P.simulate()
        run, engobj = P.emit(nc, sems, dma_sems)
        with nc.Block() as block:
            @block.tensor
            def _(e):
                run("pe", e)

            @block.scalar
            def _(e):
                run("act", e)

            @block.vector
            def _(e):
                run("dve", e)

            @block.gpsimd
            def _(e):
                run("pool", e)

            @block.sync
            def _(e):
                run("sp", e)
    return nc


def host_inputs(inp):
    f = lambda a: np.ascontiguousarray(np.asarray(a), dtype=np.float32)
    bf = lambda a: np.ascontiguousarray(np.asarray(a, dtype=np.float32).astype(ml_dtypes.bfloat16))
    sh = {}
    sh["w_in"] = f(inp["w_in"][0]); sh["w_glu"] = f(inp["w_glu"][0])
    sh["w_proj_a"] = f(inp["w_proj_a"][0]); sh["w_proj_b"] = f(inp["w_proj_b"][0])
    sh["w_out"] = f(inp["w_out"][0]); sh["w_ff1"] = f(inp["w_ff1"][0]); sh["w_ff2"] = f(inp["w_ff2"][0])
    sh["gmix_pc"] = f(np.asarray(inp["norm_mix"][0]).reshape(8, 128).T)
    sh["gffn_pc"] = f(np.asarray(inp["norm_ffn"][0]).reshape(8, 128).T)
    sh["gfin_bc"] = f(np.broadcast_to(np.asarray(inp["norm_final"])[None, :], (128, D)))
    sh["bgate_pc"] = f(np.asarray(inp["b_gate"][0]).reshape(16, 128).T)
    rb = np.asarray(inp["rel_bias"][0], dtype=np.float32)
    sh["rbpad"] = f(np.concatenate([rb, np.repeat(rb[:, -1:], 128, axis=1)], axis=1))
    sh["rbc"] = f(np.broadcast_to(rb[:, -1][None, :], (128, 8)))
    L2 = lambda a: np.asarray(a).reshape(16, 2, 64).transpose(1, 2, 0).reshape(128, 16)
    sh["are_L"] = f(L2(inp["ssm_a_re"][0])); sh["aim_L"] = f(L2(inp["ssm_a_im"][0]))
    sh["ldt_L"] = f(L2(np.broadcast_to(np.asarray(inp["ssm_log_dt"][0])[:, None], (32, 64))))
    LB = lambda a: np.asarray(a).reshape(16, 2, 64, 16).transpose(1, 2, 0, 3).reshape(128, 16, 16)
    sh["bre_L"] = f(LB(inp["ssm_b_re"][0])); sh["bim_L"] = f(LB(inp["ssm_b_im"][0]))
    LC = lambda a: np.asarray(a).reshape(16, 2, 16, 64).transpose(1, 3, 0, 2).reshape(128, 16, 16)
    sh["cre_L"] = f(LC(inp["ssm_c_re"][0])); sh["cim_L"] = f(LC(inp["ssm_c_im"][0]))
    sh["dcol"] = f(np.tile(np.asarray(inp["ssm_d"][0]).T, (8, 1)))
    sh["ident_bf"] = bf(np.eye(128)); sh["jmat_bf"] = bf(np.eye(128)[::-1])
    sh["ident_f"] = f(np.eye(128))
    rr = np.arange(128)
    sh["tmask"] = f((rr[None, :] // 16) >= (rr[:, None] // 16))
    return sh


_NC_CACHE = {}


def kernel(**inputs):
    xfull = np.asarray(inputs["x"], dtype=np.float32)
    B = xfull.shape[0]
    assert B == NCORES * SEQ_PER_CORE
    shared = host_inputs(inputs)
    if "nc" not in _NC_CACHE:
        _NC_CACHE["nc"] = build_program()
    nc = _NC_CACHE["nc"]
    in_maps = []
    for c in range(NCORES):
        m = dict(shared)
        m["x"] = np.ascontiguousarray(xfull[c * SEQ_PER_CORE:(c + 1) * SEQ_PER_CORE].reshape(SEQ_PER_CORE * SEQ, D))
        in_maps.append(m)
    res = run_bass_kernel_spmd(nc, in_maps, core_ids=list(range(NCORES)))
    outs = [np.asarray(r["out"]).reshape(SEQ_PER_CORE, SEQ, D) for r in res.results]
    return np.concatenate(outs, axis=0).astype(np.float32)
```

```python
import math
from contextlib import ExitStack

import numpy as np
import ml_dtypes
import concourse.bass as bass
import concourse.mybir as mybir
from concourse.bass_utils import run_bass_kernel_spmd

F32 = mybir.dt.float32
BF16 = mybir.dt.bfloat16
I32 = mybir.dt.int32
U8 = mybir.dt.uint8
AF = mybir.ActivationFunctionType
ALU = mybir.AluOpType

NCORES = 8
D = 1024
SEQ = 4096
TOK = 512
NT_SEQ = SEQ // TOK
SEQ_PER_CORE = 2
NTILES = NT_SEQ * SEQ_PER_CORE
EPS = 1e-6
RING = 3
CH_BYTES = 8192
NCH = 39
ESZ = {F32: 4, BF16: 2, I32: 4}
ENGS = ("pe", "act", "dve", "pool", "sp")
GROUPED = ("setup",)
DEBUG = {}


class Op:
    __slots__ = ("eng", "fn", "deps", "sig", "cnt", "dma", "dval", "seq")


class Prog:
    def __init__(self):
        self.ops = {e: [] for e in ENGS}
        self.res = {}
        self.dma_cnt = {}
        self.last_dma = {}

    def _key(self, o):
        return ("d", o.dma) if o.dma is not None else ("e", o.eng)

    def op(self, eng, fn, reads=(), writes=(), dma=None):
        o = Op()
        o.eng, o.fn, o.sig, o.cnt, o.dma, o.dval = eng, fn, False, None, dma, None
        o.seq = len(self.ops[eng])
        cand = {}

        def add(d, kind):
            if d is None:
                return
            k = self._key(d)
            if d.dma is None and d.eng == eng and dma is None:
                if eng == "pe":
                    return
            prev = cand.get(k)
            if prev is None or (d.dval if d.dma is not None else d.seq) > (prev.dval if prev.dma is not None else prev.seq):
                cand[k] = d

        for r in reads:
            st = self.res.get(r)
            if st is not None:
                add(st[0], "raw")
        for w in writes:
            st = self.res.get(w)
            if st is not None:
                add(st[0], "waw")
                for rd in st[1].values():
                    add(rd, "war")
        o.deps = list(cand.values())
        for d in o.deps:
            if d.dma is None:
                d.sig = True
        if dma is not None:
            self.dma_cnt[dma] = self.dma_cnt.get(dma, 0) + 16
            o.dval = self.dma_cnt[dma]
            self.last_dma[dma] = o
        for r in reads:
            st = self.res.get(r)
            if st is None:
                st = self.res[r] = [None, {}]
            st[1][self._key(o)] = o
        for w in writes:
            self.res[w] = [o, {}]
        self.ops[eng].append(o)
        return o

    def barrier(self):
        lasts = []
        for e in ENGS:
            for o in reversed(self.ops[e]):
                if o.dma is None and o.fn is not None:
                    lasts.append(o)
                    break
        dmas = list(self.last_dma.values())
        for e in ENGS:
            o = Op()
            o.eng, o.fn, o.sig, o.cnt, o.dma, o.dval = e, None, False, None, None, None
            o.seq = len(self.ops[e])
            o.deps = [d for d in lasts if d.eng != e] + dmas
            for d in o.deps:
                if d.dma is None:
                    d.sig = True
            self.ops[e].append(o)

    def simulate(self):
        for e in ENGS:
            c = 0
            for o in self.ops[e]:
                if o.sig:
                    c += 1
                    o.cnt = c
        pc = {e: 0 for e in ENGS}
        sem = {("e", e): 0 for e in ENGS}
        for k in self.dma_cnt:
            sem[("d", k)] = 0
        progress = True
        while progress:
            progress = False
            for e in ENGS:
                while pc[e] < len(self.ops[e]):
                    o = self.ops[e][pc[e]]
                    ok = True
                    for d in o.deps:
                        if d.dma is not None:
                            if sem[("d", d.dma)] < (self.dma_cnt[d.dma] if d.dma in GROUPED else d.dval):
                                ok = False
                        elif sem[("e", d.eng)] < d.cnt:
                            ok = False
                    if not ok:
                        break
                    if o.dma is not None:
                        sem[("d", o.dma)] += 16
                    elif o.sig:
                        sem[("e", e)] += 1
                    pc[e] += 1
                    progress = True
        stuck = {e: (pc[e], len(self.ops[e])) for e in ENGS if pc[e] < len(self.ops[e])}
        if stuck:
            msg = []
            for e, (p, n) in stuck.items():
                o = self.ops[e][p]
                msg.append((e, p, n, [(d.eng, d.dma, d.cnt, d.dval, d.seq) for d in o.deps]))
            raise RuntimeError("semaphore deadlock: %s" % msg)

    def emit(self, nc, sems, dma_sems):
        for e in ENGS:
            c = 0
            for o in self.ops[e]:
                if o.sig:
                    c += 1
                    o.cnt = c
        engobj = {"pe": nc.tensor, "act": nc.scalar, "dve": nc.vector, "pool": nc.gpsimd, "sp": nc.sync}

        def run(ename, e):
            waited = {}
            for o in self.ops[ename]:
                for d in o.deps:
                    if d.dma is not None:
                        k, v, s = ("d", d.dma), (self.dma_cnt[d.dma] if d.dma in GROUPED else d.dval), dma_sems[d.dma]
                    else:
                        k, v, s = ("e", d.eng), d.cnt, sems[d.eng]
                    if waited.get(k, 0) < v:
                        e.wait_ge(s, v)
                        waited[k] = v
                if o.fn is None:
                    continue
                inst = o.fn(e)
                if o.dma is not None:
                    inst.then_inc(dma_sems[o.dma], 16)
                elif o.sig:
                    inst.then_inc(sems[ename], 1)
            if ename in ("pool", "sp"):
                for st, tot in self.dma_cnt.items():
                    if waited.get(("d", st), 0) < tot:
                        e.wait_ge(dma_sems[st], tot)
        return run, engobj


def view(pool, off, dtype, shape):
    n = int(np.prod(shape)) * ESZ[dtype]
    v = pool[:, off:off + n].bitcast(dtype)
    if len(shape) == 1:
        return v
    names = "abcdef"[:len(shape)]
    pat = "p (" + " ".join(names) + ") -> p " + " ".join(names)
    return v.rearrange(pat, **{names[k]: int(shape[k]) for k in range(len(shape))})


class Alloc:
    def __init__(self, pool, start=0):
        self.pool, self.off = pool, start

    def __call__(self, dtype, shape, align=64):
        self.off = (self.off + align - 1) // align * align
        n = int(np.prod(shape)) * ESZ[dtype]
        v = view(self.pool, self.off, dtype, shape)
        self.off += n
        return v


CH_WIN = [0, 1, 2, 3]
CH_M, CH_T, CH_ER, CH_EI = 35, 36, 37, 38
CH_GLU = 8
CH_MIX = list(range(9, 17))
CH_WOUT = [17, 18]
CH_W1 = list(range(19, 27))
CH_W2 = list(range(27, 35))


def ffn_plan():
    plan = []
    for step in ("F1_0", "F1_1", "F2_0", "F1_2", "F2_1", "F1_3", "F2_2", "F2_3"):
        kind, r = step.split("_")
        plan.append((kind, int(r)))
    return plan


def tile_chunk_order():
    order = CH_WIN + [CH_M, CH_T, CH_ER, CH_EI, CH_GLU] + CH_MIX + CH_WOUT
    for kind, r in ffn_plan():
        if kind == "F1":
            order += [CH_W1[2 * r], CH_W1[2 * r + 1]]
        else:
            order += [CH_W2[2 * r], CH_W2[2 * r + 1]]
    return order


class _Stop(Exception):
    pass


def build_program(ntiles=NTILES, debug=None, lim=99):
    debug = debug or {}
    nc = bass.Bass("TRN2", target_bir_lowering=False)
    P = Prog()
    ntok = ntiles * TOK

    def din(name, shape, dt=F32):
        return nc.dram_tensor(name, list(shape), dt, kind="ExternalInput").ap()

    x = din("x", [SEQ_PER_CORE * SEQ, D])
    w_in = din("w_in", [D, 4096])
    w_glu = din("w_glu", [512, 1024])
    w_pa = din("w_proj_a", [512, 1024])
    w_pb = din("w_proj_b", [512, 1024])
    w_out = din("w_out", [D, D])
    w_ff1 = din("w_ff1", [D, 4096])
    w_ff2 = din("w_ff2", [4096, D])
    gmix_d = din("gmix_pc", [128, 8])
    gffn_d = din("gffn_pc", [128, 8])
    gfin_d = din("gfin_bc", [128, D])
    bgate_d = din("bgate_pc", [128, 16])
    rbpad_d = din("rbpad", [8, 385])
    rbc_d = din("rbc", [128, 8])
    are_d = din("are_L", [128, 16])
    aim_d = din("aim_L", [128, 16])
    ldt_d = din("ldt_L", [128, 16])
    bre_d = din("bre_L", [128, 16, 16])
    bim_d = din("bim_L", [128, 16, 16])
    cre_d = din("cre_L", [128, 16, 16])
    cim_d = din("cim_L", [128, 16, 16])
    dcol_d = din("dcol", [128, 32])
    identb_d = din("ident_bf", [128, 128], BF16)
    jmat_d = din("jmat_bf", [128, 128], BF16)
    identf_d = din("ident_f", [128, 128])
    tmask_d = din("tmask", [128, 128])
    out = nc.dram_tensor("out", [SEQ_PER_CORE * SEQ, D], F32, kind="ExternalOutput").ap()
    wsc = nc.dram_tensor("wsc", [NCH, 128, CH_BYTES // 2], BF16, kind="Internal").ap()
    dbg_out = {}
    for k, shp in debug.items():
        dbg_out[k] = nc.dram_tensor("dbg_" + k, list(shp), F32, kind="ExternalOutput").ap()

    es = ExitStack()
    with es:
        SB_BYTES = 212736
        pool_t = es.enter_context(nc.sbuf_tensor("sbpool", [128, SB_BYTES], U8))
        pool = pool_t[:]
        PS_t = es.enter_context(nc.psum_tensor("pspool", [128, 4096], F32))
        PS = PS_t[:]
        sems = {e: es.enter_context(nc.semaphore("sem_" + e)) for e in ENGS}
        dma_streams = ["x", "out", "cvl0", "cvl1", "cvs0", "cvs1", "cvs2", "setup", "dbg"] + ["w%d" % s for s in range(RING)]
        dma_sems = {s: es.enter_context(nc.semaphore("dsem_" + s)) for s in dma_streams}

        def psb(b, n=1):
            return PS[:, b * 512:(b + n) * 512]

        def psb16(b, n=1):
            return PS[:, b * 512:(b + n) * 512].bitcast(BF16)

        def PSR(b, n=1):
            return [("ps", b + k) for k in range(n)]

        A = Alloc(pool)
        identb = A(BF16, [128])
        jmat = A(BF16, [128])
        gfin = A(F32, [D])
        bgate = A(F32, [16])
        Hb = A(BF16, [8, 2, 128])
        CT = A(F32, [16, 65])
        ST = A(F32, [16, 65])
        RT = A(F32, [16, 65])
        CR = A(F32, [16])
        CI = A(F32, [16])
        zero1 = A(F32, [1])
        persist_end = A.off

        Xs = [A(F32, [4, D]) for _ in range(2)]
        ust = A(BF16, [4, D])
        ss = A(F32, [4])
        rstd = A(F32, [4])
        inv2 = A(F32, [4])
        uT = A(BF16, [8, TOK])
        aT2 = A(BF16, [8, TOK])
        qT = A(BF16, [4, TOK])
        kT = A(BF16, [4, 2 * TOK])
        Vaug = A(BF16, [8, 8, 128])
        gs = [A(BF16, [TOK]) for _ in range(4)]
        usb = A(BF16, [8, 512])
        Ublk = A(BF16, [32, 64])
        d1r = A(F32, [16, 65])
        d1i = A(F32, [16, 65])
        Wr = A(F32, [16, 65])
        Wi = A(F32, [16, 65])
        Sre = A(BF16, [16, 64])
        Sim = A(BF16, [16, 64])
        ysT = A(BF16, [4, TOK])
        sig = [A(BF16, [TOK]) for _ in range(2)]
        gluT = A(BF16, [4, TOK])
        attT = A(BF16, [4, TOK])
        PT = [A(BF16, [1408]) for _ in range(2)]
        rden = A(F32, [TOK])
        t1 = [A(BF16, [TOK]) for _ in range(2)]
        t2 = [A(BF16, [TOK]) for _ in range(2)]
        hid = [A(BF16, [8, TOK]) for _ in range(2)]
        dstg = view(pool, A.off - 16384, F32, [4096])
        ring = [A(BF16, [CH_BYTES // 2]) for _ in range(RING)]
        main_end = A.off
        assert main_end <= SB_BYTES, main_end

        S = Alloc(pool, persist_end)
        identf = S(F32, [128])
        tmask = S(F32, [128])
        gmix = S(F32, [8])
        gffn = S(F32, [8])
        rbc = S(F32, [8])
        dcol = S(F32, [32])
        hstg = S(F32, [16, 128])
        are = S(F32, [16]); aim = S(F32, [16]); ldt = S(F32, [16])
        bre = S(F32, [16, 16]); bim = S(F32, [16, 16]); cre = S(F32, [16, 16]); cim = S(F32, [16, 16])
        sm = {n: S(F32, [16]) for n in ("dt", "adt", "th", "th2", "mag", "rho", "kf", "msk", "sin", "cos",
                                        "lre", "lim", "rm2", "ire", "iim", "lm1", "nre", "nim", "den",
                                        "qre", "qim", "phr", "phi", "p2r", "p2i", "u1", "u2", "u3", "u4")}
        ki = S(I32, [16])
        PPR = S(F32, [16, 9]); PPI = S(F32, [16, 9])
        PNR = S(F32, [16, 8]); PNI = S(F32, [16, 8])
        bbr = S(F32, [16, 16]); bbi = S(F32, [16, 16])
        XR = S(F32, [16, 8, 16]); XI = S(F32, [16, 8, 16])
        MXR = S(F32, [16, 8, 16]); MXI = S(F32, [16, 8, 16])
        YR = S(F32, [16, 9, 16]); YI = S(F32, [16, 9, 16]); YN = S(F32, [16, 9, 16])
        c1 = S(F32, [16, 9, 16]); c2 = S(F32, [16, 9, 16])
        Tmat = S(BF16, [32, 128]); Mmat = S(BF16, [32, 2, 64]); EmR = S(BF16, [32, 128]); EmI = S(BF16, [32, 128])
        tmpT = S(F32, [128])
        stg = [S(F32, [4096]) for _ in range(2)]
        cvb = [S(BF16, [4096]) for _ in range(2)]
        assert S.off <= SB_BYTES, S.off

        def mm(out_, lhsT, rhs, start, stop, reads, writes):
            return P.op("pe", lambda e: e.matmul(out_, lhsT=lhsT, rhs=rhs, start=start, stop=stop), reads, writes)

        def tr(out_, in_, ident, reads, writes):
            return P.op("pe", lambda e: e.transpose(out_, in_, ident), reads, writes)

        def act(out_, in_, func, reads, writes, eng="act", **kw):
            return P.op(eng, lambda e: e.activation(out=out_, in_=in_, func=func, **kw), reads, writes)

        def tt(out_, in0, in1, op, reads, writes, eng="dve"):
            return P.op(eng, lambda e: e.tensor_tensor(out=out_, in0=in0, in1=in1, op=op), reads, writes)

        def tsc(out_, in0, s1, s2, op0, op1, reads, writes, eng="dve"):
            if op1 is None:
                return P.op(eng, lambda e: e.tensor_scalar(out=out_, in0=in0, scalar1=s1, scalar2=None, op0=op0), reads, writes)
            return P.op(eng, lambda e: e.tensor_scalar(out=out_, in0=in0, scalar1=s1, scalar2=s2, op0=op0, op1=op1), reads, writes)

        def stt(out_, in0, scalar, in1, op0, op1, reads, writes):
            return P.op("dve", lambda e: e.scalar_tensor_tensor(out=out_, in0=in0, scalar=scalar, in1=in1, op0=op0, op1=op1), reads, writes)

        def cp(out_, in_, reads, writes, eng="dve"):
            if eng == "act":
                return act(out_, in_, AF.Copy, reads, writes)
            return P.op(eng, lambda e: e.tensor_copy(out=out_, in_=in_), reads, writes)

        def ms(ap, val, writes, eng="dve"):
            return P.op(eng, lambda e: e.memset(ap, val), (), writes)

        def dma(out_, in_, reads, writes, stream, eng="sp"):
            return P.op(eng, lambda e: e.dma_start(out=out_, in_=in_), reads, writes, dma=stream)

        def dbg(name, ap, reads):
            if name in dbg_out:
                dma(dbg_out[name], ap, reads, (), "dbg")

        for (t_, d_, nm) in ((identb, identb_d, "identb"), (jmat, jmat_d, "jmat"), (gfin, gfin_d, "gfin"),
                             (bgate, bgate_d, "bgate"), (identf, identf_d, "identf"), (tmask, tmask_d, "tmask"),
                             (gmix, gmix_d, "gmix"), (gffn, gffn_d, "gffn"), (rbc, rbc_d, "rbc"), (dcol, dcol_d, "dcol"),
                             (are, are_d, "are"), (aim, aim_d, "aim"), (ldt, ldt_d, "ldt"),
                             (bre, bre_d, "bre"), (bim, bim_d, "bim"), (cre, cre_d, "cre"), (cim, cim_d, "cim")):
            dma(t_, d_, (), (nm,), "setup")
        ms(zero1, 0.0, ("zero1",))

        for h in range(8):
            for di, delta in enumerate((0, 128)):
                src = bass.AP(tensor=rbpad_d.tensor, offset=h * 385 + delta + 1, ap=[[1, 128], [1, 128]])
                dma(hstg[:, h * 2 + di, :], src, (), (("hstg", h, di),), "setup")
        for h in range(8):
            tsc(Hb[:, h, :, :], hstg[:, 2 * h:2 * h + 2, :], rbc[:, h:h + 1], None, ALU.subtract, None,
                [("hstg", h, 0), ("hstg", h, 1), "rbc"], [("Hb", h)])
        ms(Hb[0:64, :, 0, 0:64], -30000.0, [("Hb", h) for h in range(8)])

        SR = lambda *n: list(n)

        def smop(kind, o, a, b=None, **kw):
            rd = [a] + ([b] if isinstance(b, str) else [])
            if kind == "tt":
                tt(sm[o], sm[a], sm[b], kw["op"], rd, [o])
            elif kind == "act":
                act(sm[o], sm[a], kw["func"], rd, [o], **{k: v for k, v in kw.items() if k != "func"})
        sm["are"], sm["aim"], sm["ldt"] = are, aim, ldt
        act(sm["dt"], ldt, AF.Exp, ["ldt"], ["dt"])
        tt(sm["adt"], are, sm["dt"], ALU.mult, ["are", "dt"], ["adt"])
        tt(sm["th"], aim, sm["dt"], ALU.mult, ["aim", "dt"], ["th"])
        act(sm["mag"], sm["adt"], AF.Exp, ["adt"], ["mag"])
        act(sm["rho"], sm["adt"], AF.Exp, ["adt"], ["rho"], scale=8.0)
        tsc(sm["th2"], sm["th"], math.pi / 2, None, ALU.add, None, ["th"], ["th2"])
        TWO_PI = 2.0 * math.pi
        for src_, dst_ in (("th", "sin"), ("th2", "cos")):
            tsc(ki, sm[src_], 1.0 / TWO_PI, None, ALU.mult, None, [src_], ["ki"])
            cp(sm["kf"], ki, ["ki"], ["kf"])
            stt(sm["u1"], sm["kf"], -TWO_PI, sm[src_], ALU.mult, ALU.add, ["kf", src_], ["u1"])
            tsc(sm["msk"], sm["u1"], math.pi, None, ALU.is_gt, None, ["u1"], ["msk"])
            stt(sm["u2"], sm["msk"], -TWO_PI, sm["u1"], ALU.mult, ALU.add, ["msk", "u1"], ["u2"])
            tsc(sm["msk"], sm["u2"], -math.pi, None, ALU.is_lt, None, ["u2"], ["msk"])
            stt(sm["u3"], sm["msk"], TWO_PI, sm["u2"], ALU.mult, ALU.add, ["msk", "u2"], ["u3"])
            tsc(sm["u4"], sm["u3"], 3.14159, -3.14159, ALU.min, ALU.max, ["u3"], ["u4"])
            act(sm[dst_], sm["u4"], AF.Sin, ["u4"], [dst_])
        tt(sm["lre"], sm["mag"], sm["cos"], ALU.mult, ["mag", "cos"], ["lre"])
        tt(sm["lim"], sm["mag"], sm["sin"], ALU.mult, ["mag", "sin"], ["lim"])

        def cmul(outr, outi, ar, ai, br, bi, tmp1, tmp2, rd, wr):
            tt(tmp1, ar, br, ALU.mult, rd, ["ctmp1"])
            tt(tmp2, ai, bi, ALU.mult, rd, ["ctmp2"])
            wr_r, wr_i = (wr if isinstance(wr, tuple) else (wr + "_r", wr + "_i"))
            tt(outr, tmp1, tmp2, ALU.subtract, ["ctmp1", "ctmp2"], [wr_r])
            tt(tmp1, ar, bi, ALU.mult, rd, ["ctmp1"])
            tt(tmp2, ai, br, ALU.mult, rd, ["ctmp2"])
            tt(outi, tmp1, tmp2, ALU.add, ["ctmp1", "ctmp2"], [wr_i])

        c1s = c1[:, :, 0, :]
        c2s = c2[:, :, 0, :]
        c1v = c1s[:, :, 0]
        c2v = c2s[:, :, 0]
        ms(PPR[:, :, 0], 1.0, ["PP0_r"])
        ms(PPI[:, :, 0], 0.0, ["PP0_i"])
        cp(PPR[:, :, 1], sm["lre"], ["lre"], ["PP1_r"])
        cp(PPI[:, :, 1], sm["lim"], ["lim"], ["PP1_i"])
        for k in range(2, 9):
            cmul(PPR[:, :, k], PPI[:, :, k], PPR[:, :, k - 1], PPI[:, :, k - 1], sm["lre"], sm["lim"], c1v, c2v,
                 ["PP%d_r" % (k - 1), "PP%d_i" % (k - 1), "lre", "lim"], "PP%d" % k)
        tt(sm["rm2"], sm["mag"], sm["mag"], ALU.mult, ["mag"], ["rm2"])
        P.op("dve", lambda e: e.reciprocal(out=sm["rm2"], in_=sm["rm2"]), ["rm2"], ["rm2"])
        tt(sm["ire"], sm["lre"], sm["rm2"], ALU.mult, ["lre", "rm2"], ["ire"])
        tt(sm["iim"], sm["lim"], sm["rm2"], ALU.mult, ["lim", "rm2"], ["iim_p"])
        tsc(sm["iim"], sm["iim"], -1.0, None, ALU.mult, None, ["iim_p"], ["iim"])
        ms(PNR[:, :, 0], 1.0, ["PN0_r"])
        ms(PNI[:, :, 0], 0.0, ["PN0_i"])
        cp(PNR[:, :, 1], sm["ire"], ["ire"], ["PN1_r"])
        cp(PNI[:, :, 1], sm["iim"], ["iim"], ["PN1_i"])
        for k in range(2, 8):
            cmul(PNR[:, :, k], PNI[:, :, k], PNR[:, :, k - 1], PNI[:, :, k - 1], sm["ire"], sm["iim"], c1v, c2v,
                 ["PN%d_r" % (k - 1), "PN%d_i" % (k - 1), "ire", "iim"], "PN%d" % k)
        tsc(sm["lm1"], sm["lre"], -1.0, None, ALU.add, None, ["lre"], ["lm1"])
        tt(sm["u1"], sm["lm1"], are, ALU.mult, ["lm1", "are"], ["u1"])
        tt(sm["u2"], sm["lim"], aim, ALU.mult, ["lim", "aim"], ["u2"])
        tt(sm["nre"], sm["u1"], sm["u2"], ALU.add, ["u1", "u2"], ["nre"])
        tt(sm["u1"], sm["lim"], are, ALU.mult, ["lim", "are"], ["u1"])
        tt(sm["u2"], sm["lm1"], aim, ALU.mult, ["lm1", "aim"], ["u2"])
        tt(sm["nim"], sm["u1"], sm["u2"], ALU.subtract, ["u1", "u2"], ["nim"])
        tt(sm["u1"], are, are, ALU.mult, ["are"], ["u1"])
        tt(sm["u2"], aim, aim, ALU.mult, ["aim"], ["u2"])
        tt(sm["den"], sm["u1"], sm["u2"], ALU.add, ["u1", "u2"], ["den"])
        P.op("dve", lambda e: e.reciprocal(out=sm["den"], in_=sm["den"]), ["den"], ["den"])
        tt(sm["qre"], sm["nre"], sm["den"], ALU.mult, ["nre", "den"], ["qre"])
        tt(sm["qim"], sm["nim"], sm["den"], ALU.mult, ["nim", "den"], ["qim"])
        qr_b = sm["qre"].unsqueeze(2).broadcast_to([128, 16, 16])
        qi_b = sm["qim"].unsqueeze(2).broadcast_to([128, 16, 16])
        cmul(bbr, bbi, qr_b, qi_b, bre, bim, c1s, c2s, ["qre", "qim", "bre", "bim"], "bb")
        sh4 = [128, 16, 8, 16]
        c1x = c1[:, :, 0:8, :]
        c2x = c2[:, :, 0:8, :]
        pnr_b = PNR.unsqueeze(3).broadcast_to(sh4)
        pni_b = PNI.unsqueeze(3).broadcast_to(sh4)
        bbr_b = bbr.unsqueeze(2).broadcast_to(sh4)
        bbi_b = bbi.unsqueeze(2).broadcast_to(sh4)
        pn_r = ["PN%d_%s" % (k, c) for k in range(8) for c in "ri"]
        cmul(XR, XI, pnr_b, pni_b, bbr_b, bbi_b, c1x, c2x, pn_r + ["bb_r", "bb_i"], "X")
        p7r_b = PPR[:, :, 7:8].unsqueeze(3).broadcast_to(sh4)
        p7i_b = PPI[:, :, 7:8].unsqueeze(3).broadcast_to(sh4)
        cmul(MXR, MXI, p7r_b, p7i_b, XR, XI, c1x, c2x, ["PP7_r", "PP7_i", "X_r", "X_i"], "MX")
        sh9 = [128, 16, 9, 16]
        ppr_b = PPR.unsqueeze(3).broadcast_to(sh9)
        ppi_b = PPI.unsqueeze(3).broadcast_to(sh9)
        cre_b = cre.unsqueeze(2).broadcast_to(sh9)
        cim_b = cim.unsqueeze(2).broadcast_to(sh9)
        pp_r = ["PP%d_%s" % (k, c) for k in range(9) for c in "ri"]
        cmul(YR, YI, ppr_b, ppi_b, cre_b, cim_b, c1, c2, pp_r + ["cre", "cim"], "Y")
        tsc(YN, YI, -1.0, None, ALU.mult, None, ["Y_i"], ["YN"])
        ms(EmR.rearrange("p g n -> p (g n)"), 0.0, ["Emat_r"])
        ms(EmI.rearrange("p g n -> p (g n)"), 0.0, ["Emat_i"])
        for j in range(2):
            pr = slice(64 * j, 64 * j + 64)
            cp(EmR[pr, j::2, :].rearrange("p g (t h) -> p g t h", t=8), YR[pr, :, 1:9, :], ["Y_r", "Emat_r"], ["Emat_r"])
            cp(EmI[pr, j::2, :].rearrange("p g (t h) -> p g t h", t=8), YN[pr, :, 1:9, :], ["YN", "Emat_i"], ["Emat_i"])
        for g in range(32):
            gp, j = g // 2, g % 2
            pr = slice(64 * j, 64 * j + 64)
            bk = g % 4
            o_ = psb(bk)[:, 0:128]
            mm(o_, XR[pr, gp, :, :].rearrange("p t h -> p (t h)"), YR[pr, gp, 0:8, :].rearrange("p t h -> p (t h)"),
               True, False, ["X_r", "Y_r"], PSR(bk))
            mm(o_, XI[pr, gp, :, :].rearrange("p t h -> p (t h)"), YN[pr, gp, 0:8, :].rearrange("p t h -> p (t h)"),
               False, True, ["X_i", "YN"], PSR(bk))
            tt(tmpT, o_, tmask, ALU.mult, PSR(bk) + ["tmask"], ["tmpT"])
            stt(Tmat[:, g, :], identf, dcol[:, g:g + 1], tmpT, ALU.mult, ALU.add, ["identf", "dcol", "tmpT"], [("Tmat", g)])
            bk2 = 4 + g % 4
            o2 = psb(bk2)
            tr(o2[:, 0:64], MXR[pr, gp, :, :].rearrange("p t h -> p (t h)"), identf[pr, 64 * j:64 * j + 64], ["MX_r", "identf"], PSR(bk2))
            tr(o2[:, 64:128], MXI[pr, gp, :, :].rearrange("p t h -> p (t h)"), identf[pr, 64 * j:64 * j + 64], ["MX_i", "identf"], PSR(bk2))
            cp(Mmat[:, g, :, :].rearrange("p a b -> p (a b)"), o2[:, 0:128], PSR(bk2), [("Mmat", g)], eng="act")
        tsc(sm["u1"], sm["rho"], 1.0, None, ALU.mult, None, ["rho"], ["u1"])
        P.op("dve", lambda e: e.reciprocal(out=sm["u1"], in_=sm["u1"]), ["u1"], ["u1"])
        tt(sm["phr"], PPR[:, :, 8], sm["u1"], ALU.mult, ["PP8_r", "u1"], ["phr"])
        tt(sm["phi"], PPI[:, :, 8], sm["u1"], ALU.mult, ["PP8_i", "u1"], ["phi"])
        ms(CT[:, :, 0], 1.0, ["CT"])
        ms(ST[:, :, 0], 0.0, ["ST"])
        cp(CT[:, :, 1], sm["phr"], ["phr", "CT"], ["CT"])
        cp(ST[:, :, 1], sm["phi"], ["phi", "ST"], ["ST"])
        cp(sm["p2r"], sm["phr"], ["phr"], ["p2r"])
        cp(sm["p2i"], sm["phi"], ["phi"], ["p2i"])
        n = 1
        while n < 64:
            cmul(sm["u3"], sm["u4"], sm["p2r"], sm["p2i"], sm["p2r"], sm["p2i"], c1v, c2v, ["p2r", "p2i"], "sq")
            cp(sm["p2r"], sm["u3"], ["sq_r"], ["p2r"])
            cp(sm["p2i"], sm["u4"], ["sq_i"], ["p2i"])
            n *= 2
            hi = min(2 * n, 65)
            w = hi - n
            shw = [128, 16, w]
            cmul(CT[:, :, n:hi], ST[:, :, n:hi], CT[:, :, 0:w], ST[:, :, 0:w],
                 sm["p2r"].unsqueeze(2).broadcast_to(shw), sm["p2i"].unsqueeze(2).broadcast_to(shw),
                 c1.rearrange("p a b c -> p (a b c)")[:, 0:16 * w].rearrange("p (a b) -> p a b", a=16),
                 c2.rearrange("p a b c -> p (a b c)")[:, 0:16 * w].rearrange("p (a b) -> p a b", a=16),
                 ["CT", "ST", "p2r", "p2i"], ("CT", "ST"))
        cp(RT, sm["rho"].unsqueeze(2).broadcast_to([128, 16, 65]), ["rho"], ["RT"])
        ms(RT[:, :, 0], 0.0, ["RT"])

        def kview(w, cols, kc):
            return w.rearrange("(k p) n -> p k n", p=128)[:, :, cols[0]:cols[1]]

        conv = []
        for cb in range(8):
            conv.append((cb, [(0, kview(w_in, (cb * 512, cb * 512 + 512), 8), 8, 512, "gmix")]))
        conv.append((CH_GLU, [(0, kview(w_glu, (0, 1024), 4), 4, 1024, None)]))
        for m in range(8):
            conv.append((CH_MIX[m], [
                (0, kview(w_pa, (m * 128, m * 128 + 128), 4), 4, 128, None),
                (512, kview(w_pb, (m * 128, m * 128 + 128), 4), 4, 128, None),
                (1024, kview(w_in, (2048 + m * 128, 2048 + m * 128 + 128), 8), 8, 128, "gmix"),
                (2048, kview(w_in, (3072 + m * 128, 3072 + m * 128 + 128), 8), 8, 128, "gmix")]))
        for hf in range(2):
            conv.append((CH_WOUT[hf], [(0, kview(w_out, (hf * 512, hf * 512 + 512), 8), 8, 512, None)]))
        for cb in range(8):
            conv.append((CH_W1[cb], [(0, kview(w_ff1, (cb * 512, cb * 512 + 512), 8), 8, 512, "gffn")]))
        for r in range(4):
            for hf in range(2):
                conv.append((CH_W2[2 * r + hf],
                             [(0, kview(w_ff2[r * 1024:(r + 1) * 1024, :], (hf * 512, hf * 512 + 512), 8), 8, 512, None)]))
        for ci, (ch, parts) in enumerate(conv):
            sb, cb_ = stg[ci % 2], cvb[ci % 2]
            sres, cres = ("stg", ci % 2), ("cvb", ci % 2)
            tot = 0
            for (off, src, kc, ncol, sc) in parts:
                dv = sb[:, off:off + kc * ncol].rearrange("p (k n) -> p k n", k=kc)
                dma(dv, src, (), [sres], "cvl%d" % (ci % 2))
                tot = max(tot, off + kc * ncol)
            for pi, (off, src, kc, ncol, sc) in enumerate(parts):
                sv = sb[:, off:off + kc * ncol].rearrange("p (k n) -> p k n", k=kc)
                ov = cb_[:, off:off + kc * ncol].rearrange("p (k n) -> p k n", k=kc)
                if sc is None:
                    cp(ov, sv, [sres], [cres], eng=("act" if ci % 2 == 0 else "dve"))
                else:
                    gv = (gmix if sc == "gmix" else gffn).unsqueeze(2).broadcast_to([128, kc, ncol])
                    tt(ov, sv, gv, ALU.mult, [sres, sc], [cres], eng=("dve" if ci % 3 else "pool"))
            dma(wsc[ch, :, 0:tot], cb_[:, 0:tot], [cres], [("wsc", ch)], "cvs%d" % (ci % 2))
        dma(wsc[CH_M], Mmat.rearrange("p a b c -> p (a b c)"), [("Mmat", g) for g in range(32)], [("wsc", CH_M)], "cvs2")
        dma(wsc[CH_T], Tmat.rearrange("p a b -> p (a b)"), [("Tmat", g) for g in range(32)], [("wsc", CH_T)], "cvs2")
        dma(wsc[CH_ER], EmR.rearrange("p a b -> p (a b)"), ["Emat_r"], [("wsc", CH_ER)], "cvs2")
        dma(wsc[CH_EI], EmI.rearrange("p a b -> p (a b)"), ["Emat_i"], [("wsc", CH_EI)], "cvs2")
        if "Tmat" in dbg_out:
            cp(stg[0], Tmat.rearrange("p a b -> p (a b)"), [("Tmat", g) for g in range(32)], [("stg", 0)])
            dbg("Tmat", stg[0], [("stg", 0)])
        if "Mmat" in dbg_out:
            cp(stg[1], Mmat.rearrange("p a b c -> p (a b c)"), [("Mmat", g) for g in range(32)], [("stg", 1)])
            dbg("Mmat", stg[1], [("stg", 1)])
        if "tabs" in dbg_out:
            dbg("tabs", CT.rearrange("p a b -> p (a b)"), ["CT"])

        P.barrier()

        ms(Vaug[:, :, :, 64:128], 1.0, ["Vones"], eng="pool")

        order = tile_chunk_order()
        nper = len(order)
        wstate = {"n": 0, "loaded": 0}
        total_chunks = ntiles * nper

        def wnext(expect, keep=0):
            n = wstate["n"]
            assert order[n % nper] == expect, (order[n % nper], expect)
            while wstate["loaded"] < min(n - keep + RING, total_chunks):
                k = wstate["loaded"]
                s_ = k % RING
                ch_ = order[k % nper]
                L_ = 3072 if ch_ in CH_MIX else CH_BYTES // 2
                dma(ring[s_][:, 0:L_], wsc[ch_, :, 0:L_], [("wsc", ch_)], [("ring", s_)], "w%d" % s_)
                wstate["loaded"] += 1
            wstate["n"] += 1
            return ring[n % RING], ("ring", n % RING)

        bank_rr = {"i": 0}

        def nbank(lst=(0, 1, 2, 3, 4, 5, 6, 7)):
            b_ = lst[bank_rr["i"] % len(lst)]
            bank_rr["i"] += 1
            return b_

        def XR_(xb, ts=None):
            return [("X", xb, t) for t in range(4)] if ts is None else [("X", xb, ts)]

        def rms_stats(X, xb, tss=(0, 1, 2, 3)):
            for ts in tss:
                act(ust[:, ts, :], X[:, ts, :], AF.Square, XR_(xb, ts), [("ust", ts), ("ss", ts)], accum_out=ss[:, ts:ts + 1])
                r_ = rstd[:, ts:ts + 1]
                tsc(r_, ss[:, ts:ts + 1], 1.0 / D, EPS, ALU.mult, ALU.add, [("ss", ts)], [("rstd", ts)])
                act(r_, r_, AF.Sqrt, [("rstd", ts)], [("rstd", ts)])
                P.op("dve", lambda e, r_=r_: e.reciprocal(out=r_, in_=r_), [("rstd", ts)], [("rstd", ts)])

        def norm_scale(X, xb, ts):
            rms_stats(X, xb, (ts,))
            act(ust[:, ts, :], X[:, ts, :], AF.Copy, XR_(xb, ts) + [("rstd", ts)], [("ust", ts)], scale=rstd[:, ts:ts + 1])

        def norm_tr(ts, dst, dst_res, banks):
            b_ = nbank(banks)
            pv = psb16(b_)
            for kc in range(8):
                tr(pv[:, kc * 128:(kc + 1) * 128], ust[:, ts, kc * 128:(kc + 1) * 128], identb, [("ust", ts)], PSR(b_))
            cp(dst[:, :, ts * 128:(ts + 1) * 128], pv.rearrange("p (k t) -> p k t", k=8), PSR(b_), [dst_res(ts)],
               eng=("act" if ts % 2 else "dve"))

        def rmsnorm_to_T(X, xb, dst, dst_res, banks):
            for ts in range(4):
                norm_scale(X, xb, ts)
                norm_tr(ts, dst, dst_res, banks)

        def pre(i):
            xb = i % 2
            X = Xs[xb]
            xrows = x[i * TOK:(i + 1) * TOK, :].rearrange("(ts p) d -> p ts d", p=128)
            dma(X, xrows, (), XR_(xb), "x")
            rmsnorm_to_T(X, xb, uT, lambda ts: "uT", (0, 1))

        def body(i):
            ti = i % NT_SEQ
            par = ti % 2
            xb = i % 2
            X = Xs[xb]
            wq, rq = wnext(0)
            wqv = wq.rearrange("p (k n) -> p k n", k=8)
            for m in range(4):
                b = nbank((2, 3, 4, 5))
                for kc in range(8):
                    mm(psb(b), wqv[:, kc, m * 128:(m + 1) * 128], uT[:, kc, :], kc == 0, kc == 7, [rq, "uT"], PSR(b))
                act(qT[:, m, :], psb(b), AF.Copy, PSR(b), [("qT", m)], scale=0.125)
            wk, rk = wnext(1)
            wkv = wk.rearrange("p (k n) -> p k n", k=8)
            for m in range(4):
                b = nbank((2, 3, 4, 5))
                for kc in range(8):
                    mm(psb(b), wkv[:, kc, m * 128:(m + 1) * 128], uT[:, kc, :], kc == 0, kc == 7, [rk, "uT"], PSR(b))
                cp(kT[:, m, par * TOK:(par + 1) * TOK], psb(b), PSR(b), [("kT", par, m)], eng=("dve" if m % 2 else "act"))
            wv, rv = wnext(2)
            wvv = wv.rearrange("p (k n) -> p k n", k=8)
            for ts in range(4):
                b = nbank((2, 3, 4, 5))
                for kc in range(8):
                    mm(psb(b), uT[:, kc, ts * 128:(ts + 1) * 128], wvv[:, kc, :], kc == 0, kc == 7, [rv, "uT"], PSR(b))
                slot = par * 4 + ts
                pv8 = psb(b).rearrange("p (h d) -> p h d", h=8)
                cp(Vaug[:, slot, :, 0:64], pv8, PSR(b), [("V", slot)], eng="dve")
            wu, ru = wnext(3)
            wuv = wu.rearrange("p (k n) -> p k n", k=8)
            usbg = usb.rearrange("p t n -> p (t n)").rearrange("p (g t h) -> p g t h", g=32, t=8)
            for a in range(4):
                b = nbank((2, 3, 4, 5))
                for tl in range(2):
                    for kc in range(8):
                        lh = uT[:, kc, 2 * a + tl::8]
                        mm(psb(b)[64 * tl:64 * tl + 64, :], lh, wuv[:, kc, :], kc == 0, kc == 7, [ru, "uT"], PSR(b))
                ev = "act" if a % 2 == 0 else "dve"
                cp(usbg[0:64, :, 2 * a, :], psb(b)[0:64, :].rearrange("c (g h) -> c g h", g=32), PSR(b), [("usb", 2 * a)], eng=ev)
                cp(usbg[0:64, :, 2 * a + 1, :], psb(b)[64:128, :].rearrange("c (g h) -> c g h", g=32), PSR(b), [("usb", 2 * a + 1)], eng=ev)

            wM, rM = wnext(CH_M)
            Mv = wM.rearrange("p (g a b) -> p g a b", g=32, a=2)
            usbres = [("usb", t) for t in range(8)]
            pU = psb16(6, 2)
            for g in range(32):
                tr(pU[:, g * 64:(g + 1) * 64], usb.rearrange("p t n -> p (t n)")[0:64, g * 128:(g + 1) * 128], identb[0:64, 0:64],
                   usbres, PSR(6 + g // 16))
            cp(Ublk[:, 0:16, :].rearrange("p a b -> p (a b)"), pU[:, 0:1024], PSR(6), ["Ublk0"], eng="act")
            cp(Ublk[:, 16:32, :].rearrange("p a b -> p (a b)"), pU[:, 1024:2048], PSR(7), ["Ublk1"], eng="dve")
            pR = psb(2, 2).rearrange("p (g c) -> p g c", g=16)
            pI = psb(4, 2).rearrange("p (g c) -> p g c", g=16)
            for g in range(32):
                gp, j = g // 2, g % 2
                pr = slice(64 * j, 64 * j + 64)
                ub = "Ublk%d" % (g // 16)
                mm(pR[pr, gp, :], Mv[:, g, 0, :], Ublk[:, g, :], True, True, [rM, ub], PSR(2 + gp // 8))
                mm(pI[pr, gp, :], Mv[:, g, 1, :], Ublk[:, g, :], True, True, [rM, ub], PSR(4 + gp // 8))
            if ti == 0:
                ms(CR, 0.0, ["CR"])
                ms(CI, 0.0, ["CI"])
            RR, RI_ = PSR(2, 2), PSR(4, 2)
            tt(Wr[:, :, 0:64], pR, CT[:, :, 1:65], ALU.mult, RR + ["CT"], ["Wr"])
            tt(Wi[:, :, 0:64], pI, ST[:, :, 1:65], ALU.mult, RI_ + ["ST"], ["Wi"])
            tt(d1r[:, :, 1:65], Wr[:, :, 0:64], Wi[:, :, 0:64], ALU.add, ["Wr", "Wi"], ["d1r"])
            tt(Wr[:, :, 0:64], pI, CT[:, :, 1:65], ALU.mult, RI_ + ["CT"], ["Wr"])
            tt(Wi[:, :, 0:64], pR, ST[:, :, 1:65], ALU.mult, RR + ["ST"], ["Wi"])
            tt(d1i[:, :, 1:65], Wr[:, :, 0:64], Wi[:, :, 0:64], ALU.subtract, ["Wr", "Wi"], ["d1i"])
            cp(d1r[:, :, 0], CR, ["CR", "d1r"], ["d1r"])
            cp(d1i[:, :, 0], CI, ["CI", "d1i"], ["d1i"])
            def scan_part():
                fl = lambda a_: a_.rearrange("p a b -> p (a b)")
                P.op("dve", lambda e: e.tensor_tensor_scan(out=fl(Wr), data0=fl(RT), data1=fl(d1r), initial=0.0, op0=ALU.mult, op1=ALU.add),
                     ["RT", "d1r"], ["Wr"])
                P.op("dve", lambda e: e.tensor_tensor_scan(out=fl(Wi), data0=fl(RT), data1=fl(d1i), initial=0.0, op0=ALU.mult, op1=ALU.add),
                     ["RT", "d1i"], ["Wi"])
            def scan_part2():
                tA, tB = d1r, d1i
                tt(tA, Wr, CT, ALU.mult, ["Wr", "CT"], ["d1r"])
                tt(tB, Wi, ST, ALU.mult, ["Wi", "ST"], ["d1i"])
                tt(Sre, tA[:, :, 0:64], tB[:, :, 0:64], ALU.subtract, ["d1r", "d1i"], ["Sre"])
                tt(CR, tA[:, :, 64], tB[:, :, 64], ALU.subtract, ["d1r", "d1i"], ["CR"])
            def scan_part3():
                tA, tB = d1r, d1i
                tt(tA, Wr, ST, ALU.mult, ["Wr", "ST"], ["d1r"])
                tt(tB, Wi, CT, ALU.mult, ["Wi", "CT"], ["d1i"])
                tt(Sim, tA[:, :, 0:64], tB[:, :, 0:64], ALU.add, ["d1r", "d1i"], ["Sim"])
                tt(CI, tA[:, :, 64], tB[:, :, 64], ALU.add, ["d1r", "d1i"], ["CI"])


            J0 = ti * 4
            OFFS = (0, 512, 1024, 1280)
            groups = {"A": [kt for kt in range(J0 - 4, J0) if kt >= 0], "B": list(range(J0, J0 + 4))}
            GB = {"A": 0, "B": 3}
            GP = {"A": 0, "B": 1}

            def kt_info(kt):
                qa, qb = max(kt, J0), min(kt + 4, J0 + 3)
                nq = qb - qa + 1
                return qa, qb, nq, OFFS[4 - nq]

            def att_S(h, gname):
                hp = slice(64 * (h % 2), 64 * (h % 2) + 64)
                chh = h // 2
                gb = GB[gname]
                sreg = psb(gb, 3)
                pt, ptr = PT[GP[gname]], ("PT", GP[gname])
                kts = groups[gname]
                if not kts:
                    return
                used = set()
                for kt in sorted(kts, key=lambda k_: -kt_info(k_)[2]):
                    qa, qb, nq, off = kt_info(kt)
                    kpar = (kt // 4) % 2
                    kcol = kpar * TOK + (kt % 4) * 128
                    bankr = gb + off // 512
                    used.add(off // 512)
                    o_ = sreg[:, off:off + nq * 128]
                    nb = [qt for qt in (kt, kt + 1) if qa <= qt <= qb]
                    mm(o_, kT[hp, chh, kcol:kcol + 128], qT[hp, chh, (qa - J0) * 128:(qb - J0 + 1) * 128], True, not nb,
                       [("kT", kpar, chh), ("qT", chh)], PSR(bankr))
                    for bi, qt in enumerate(nb):
                        blk = sreg[:, off + (qt - qa) * 128: off + (qt - qa + 1) * 128]
                        mm(blk, jmat, Hb[:, h, qt - kt, :], False, bi == len(nb) - 1, ["jmat", ("Hb", h)], PSR(bankr))

            def att_exp(h, gname):
                gb = GB[gname]
                sreg = psb(gb, 3)
                pt, ptr = PT[GP[gname]], ("PT", GP[gname])
                kts = groups[gname]
                if not kts:
                    return
                spans = {0: (0, 512), 1: (512, 896), 2: (1024, 1408)}
                have = {0: any(kt_info(k_)[2] == 4 for k_ in kts), 1: any(kt_info(k_)[2] == 3 for k_ in kts)}
                n2 = [k_ for k_ in kts if kt_info(k_)[2] <= 2]
                for bk in (0, 1):
                    if have[bk]:
                        lo_, hi2 = spans[bk]
                        act(pt[:, lo_:hi2], sreg[:, lo_:hi2], AF.Exp, PSR(gb + bk), [(ptr, bk)])
                if n2:
                    lo_ = min(kt_info(k_)[3] for k_ in n2)
                    hi2 = max(kt_info(k_)[3] + kt_info(k_)[2] * 128 for k_ in n2)
                    act(pt[:, lo_:hi2], sreg[:, lo_:hi2], AF.Exp, PSR(gb + 2), [(ptr, 2)])
                if gname == "A":
                    for kt in kts:
                        qa, qb, nq, off = kt_info(kt)
                        c_ = off + (nq - 1) * 128 + 64
                        ms(pt[0:64, c_:c_ + 64], 0.0, [(ptr, min(off // 512, 2))], eng="pool")

            def att_PV(h):
                chh = h // 2
                ab = 6 + h % 2
                ao = psb(ab)
                allk = [("B", kt) for kt in groups["B"]] + [("A", kt) for kt in groups["A"]]
                for idx, (gname, kt) in enumerate(allk):
                    qa, qb, nq, off = kt_info(kt)
                    pt, ptr = PT[GP[gname]], ("PT", GP[gname])
                    slot = ((kt // 4) % 2) * 4 + kt % 4
                    vw = Vaug[:, slot, :, :].rearrange("p h d -> p (h d)")
                    c0 = h * 128 - 64 * (h % 2)
                    mm(ao[:, (qa - J0) * 128:(qb - J0 + 1) * 128], vw[:, c0:c0 + 128], pt[:, off:off + nq * 128],
                       idx == 0, idx == len(allk) - 1, [("V", slot), "Vones", (ptr, min(off // 512, 2))], PSR(ab))
                lo, hi_ = (slice(0, 64), slice(64, 128)) if h % 2 == 0 else (slice(64, 128), slice(0, 64))
                P.op("dve", lambda e: e.reciprocal(out=rden[lo, :], in_=ao[hi_, :]), PSR(ab), ["rden"])
                tt(attT[lo, chh, :], ao[lo, :], rden[lo, :], ALU.mult, PSR(ab) + ["rden"], [("attT", chh)])

            for h in range(8):
                att_S(h, "B")
                att_S(h, "A")
                if h >= 1:
                    att_PV(h - 1)
                att_exp(h, "B")
                att_exp(h, "A")
                if h == 1:
                    scan_part()
                elif h == 3:
                    scan_part2()
                elif h == 5:
                    scan_part3()
            att_PV(7)

            wT, rT = wnext(CH_T)
            Tv = wT.rearrange("p (g n) -> p g n", g=32)
            wER, rER = wnext(CH_ER, keep=1)
            ERv = wER.rearrange("p (g n) -> p g n", g=32)
            wEI, rEI = wnext(CH_EI, keep=2)
            EIv = wEI.rearrange("p (g n) -> p g n", g=32)
            for qd in range(4):
                yb = 0 if qd % 2 == 0 else 2
                pY = psb(yb, 2)
                for gl in range(8):
                    g = qd * 8 + gl
                    gp = g // 2
                    o_ = pY[0:64, gl * 128:(gl + 1) * 128]
                    bb_ = PSR(yb + gl // 4)
                    mm(o_, Ublk[:, g, :], Tv[:, g, :], True, False, [rT, "Ublk%d" % (g // 16)], bb_)
                    mm(o_, Sre[:, gp, :], ERv[:, g, :], False, False, [rER, "Sre"], bb_)
                    mm(o_, Sim[:, gp, :], EIv[:, g, :], False, True, [rEI, "Sim"], bb_)
                act(usb[0:64, :, qd * 128:(qd + 1) * 128].rearrange("c t (g h) -> c t g h", g=8),
                    pY[0:64, :].rearrange("c (g t h) -> c t g h", g=8, t=8), AF.Gelu_apprx_tanh,
                    PSR(yb, 2), [("usb", t) for t in range(8)])
            pZ = psb16(4, 2)
            for fc in range(4):
                for t in range(8):
                    col = (fc * 8 + t) * 64
                    tr(pZ[:, col:col + 64], usb[0:64, t, fc * 128:(fc + 1) * 128], identb[0:64, 0:64], usbres, PSR(4 + col // 1024))
            for half in range(2):
                cp(ysT[:, 2 * half:2 * half + 2, :].rearrange("p f (c t) -> p f t c", t=8),
                   pZ[:, half * 1024:(half + 1) * 1024].rearrange("p (f t c) -> p f t c", f=2, t=8), PSR(4 + half),
                   [("ysT", half)], eng=("act" if half else "dve"))

            wg, rg = wnext(CH_GLU)
            wgv = wg.rearrange("p (k n) -> p k n", k=4)
            for m in range(4):
                b1 = nbank()
                for kc in range(4):
                    mm(psb(b1), wgv[:, kc, 512 + m * 128:512 + (m + 1) * 128], ysT[:, kc, :], kc == 0, kc == 3, [rg, ("ysT", kc // 2)], PSR(b1))
                act(sig[m % 2], psb(b1), AF.Sigmoid, PSR(b1), [("sig", m % 2)])
                b2 = nbank()
                for kc in range(4):
                    mm(psb(b2), wgv[:, kc, m * 128:(m + 1) * 128], ysT[:, kc, :], kc == 0, kc == 3, [rg, ("ysT", kc // 2)], PSR(b2))
                tt(gluT[:, m, :], psb(b2), sig[m % 2], ALU.mult, PSR(b2) + [("sig", m % 2)], [("gluT", m)])

            for m in range(8):
                wm, rm = wnext(CH_MIX[m])
                wa = wm[:, 0:512].rearrange("p (k n) -> p k n", k=4)
                wb = wm[:, 512:1024].rearrange("p (k n) -> p k n", k=4)
                wga = wm[:, 1024:2048].rearrange("p (k n) -> p k n", k=8)
                wgb = wm[:, 2048:3072].rearrange("p (k n) -> p k n", k=8)
                bga, bgb, bya, byb = nbank(), nbank(), nbank(), nbank()
                for kc in range(8):
                    mm(psb(bga), wga[:, kc, :], uT[:, kc, :], kc == 0, kc == 7, [rm, "uT"], PSR(bga))
                ga = gs[(2 * m) % 4]
                act(ga, psb(bga), AF.Sigmoid, PSR(bga), [("gs", (2 * m) % 4)], bias=bgate[:, m:m + 1])
                for kc in range(8):
                    mm(psb(bgb), wgb[:, kc, :], uT[:, kc, :], kc == 0, kc == 7, [rm, "uT"], PSR(bgb))
                gb = gs[(2 * m + 1) % 4]
                act(gb, psb(bgb), AF.Sigmoid, PSR(bgb), [("gs", (2 * m + 1) % 4)], bias=bgate[:, 8 + m:9 + m])
                for kc in range(4):
                    mm(psb(bya), wa[:, kc, :], attT[:, kc, :], kc == 0, kc == 3, [rm, ("attT", kc)], PSR(bya))
                for kc in range(4):
                    mm(psb(byb), wb[:, kc, :], gluT[:, kc, :], kc == 0, kc == 3, [rm, ("gluT", kc)], PSR(byb))
                tt(t1[m % 2], psb(bya), ga, ALU.mult, PSR(bya) + [("gs", (2 * m) % 4)], [("t1", m % 2)])
                tt(t2[m % 2], psb(byb), gb, ALU.mult, PSR(byb) + [("gs", (2 * m + 1) % 4)], [("t2", m % 2)])
                tt(aT2[:, m, :], t1[m % 2], t2[m % 2], ALU.add, [("t1", m % 2), ("t2", m % 2)], [("aT2", t_) for t_ in range(4)], eng="pool")

            wo0, ro0 = wnext(CH_WOUT[0])
            wo1, ro1 = wnext(CH_WOUT[1], keep=1)
            wovs = [(wo0.rearrange("p (k n) -> p k n", k=8), ro0), (wo1.rearrange("p (k n) -> p k n", k=8), ro1)]
            for ts in range(4):
                for hf in range(2):
                    wov, ro = wovs[hf]
                    b = nbank()
                    for kc in range(8):
                        mm(psb(b), aT2[:, kc, ts * 128:(ts + 1) * 128], wov[:, kc, :], kc == 0, kc == 7, [ro, ("aT2", ts)], PSR(b))
                    xs = X[:, ts, hf * 512:(hf + 1) * 512]
                    tt(xs, psb(b), xs, ALU.add, PSR(b) + XR_(xb, ts), XR_(xb, ts))
                act(ust[:, ts, :], X[:, ts, :], AF.Square, XR_(xb, ts), [("ust", ts), ("ss", ts)], accum_out=ss[:, ts:ts + 1])
                act(ust[:, ts, :], X[:, ts, :], AF.Copy, XR_(xb, ts), [("ust", ts)])
                tsc(inv2[:, ts:ts + 1], ss[:, ts:ts + 1], 1.0 / D, EPS, ALU.mult, ALU.add, [("ss", ts)], [("inv2", ts)])
                P.op("dve", lambda e, ts=ts: e.reciprocal(out=inv2[:, ts:ts + 1], in_=inv2[:, ts:ts + 1]), [("inv2", ts)], [("inv2", ts)])
                if ts >= 1:
                    norm_tr(ts - 1, aT2, lambda t_: ("aT2", t_), (0, 1))
            norm_tr(3, aT2, lambda t_: ("aT2", t_), (0, 1))

        def ffn_post(i):
            xb = i % 2
            X = Xs[xb]
            for step, (kind, r) in enumerate(ffn_plan()):
                hb = hid[r % 2]
                if kind == "F1":
                    for c2_ in range(2):
                        w1, r1 = wnext(CH_W1[2 * r + c2_])
                        w1v = w1.rearrange("p (k n) -> p k n", k=8)
                        for mq in range(4):
                            mmi = c2_ * 4 + mq
                            b = nbank((0, 1, 2, 3))
                            for kc in range(8):
                                mm(psb(b), w1v[:, kc, mq * 128:(mq + 1) * 128], aT2[:, kc, :], kc == 0, kc == 7, [r1] + [("aT2", t_) for t_ in range(4)], PSR(b))
                            hres = ("hid", r % 2, mmi)
                            if mmi % 2 == 0:
                                act(hb[:, mmi, :], psb(b), AF.Relu, PSR(b), [hres])
                            else:
                                tsc(hb[:, mmi, :], psb(b), 0.0, None, ALU.max, None, PSR(b), [hres])
                            tt(hb[:, mmi, :], hb[:, mmi, :], hb[:, mmi, :], ALU.mult, [hres], [hres], eng="pool")
                else:
                    for hf in range(2):
                        w2, r2 = wnext(CH_W2[2 * r + hf])
                        w2v = w2.rearrange("p (k n) -> p k n", k=8)
                        for ts in range(4):
                            b = nbank((4, 5, 6, 7))
                            for ks in range(8):
                                mm(psb(b), hb[:, ks, ts * 128:(ts + 1) * 128], w2v[:, ks, :], ks == 0, ks == 7,
                                   [r2, ("hid", r % 2, ks)], PSR(b))
                            xs = X[:, ts, hf * 512:(hf + 1) * 512]
                            stt(xs, psb(b), inv2[:, ts:ts + 1], xs, ALU.mult, ALU.add, PSR(b) + XR_(xb, ts) + [("inv2", ts)], XR_(xb, ts))
                if step == 3 and i + 1 < ntiles:
                    pre(i + 1)
            rms_stats(X, xb)
            for ts in range(4):
                stt(X[:, ts, :], X[:, ts, :], rstd[:, ts:ts + 1], gfin, ALU.mult, ALU.mult, XR_(xb, ts) + [("rstd", ts), "gfin"], XR_(xb, ts))
            orows = out[i * TOK:(i + 1) * TOK, :].rearrange("(ts p) d -> p ts d", p=128)
            dma(orows, X, XR_(xb), (), "out")

        if ntiles > 0:
            pre(0)
        for i in range(ntiles):
            body(i)
            ffn_post(i)

        P.simulate()
        run, engobj = P.emit(nc, sems, dma_sems)
        with nc.Block() as block:
            @block.tensor
            def _(e):
                run("pe", e)

            @block.scalar
            def _(e):
                run("act", e)

            @block.vector
            def _(e):
                run("dve", e)

            @block.gpsimd
            def _(e):
                run("pool", e)

            @block.sync
            def _(e):
                run("sp", e)
    return nc


def host_inputs(inp):
    f = lambda a: np.ascontiguousarray(np.asarray(a), dtype=np.float32)
    bf = lambda a: np.ascontiguousarray(np.asarray(a, dtype=np.float32).astype(ml_dtypes.bfloat16))
    sh = {}
    sh["w_in"] = f(inp["w_in"][0]); sh["w_glu"] = f(inp["w_glu"][0])
    sh["w_proj_a"] = f(inp["w_proj_a"][0]); sh["w_proj_b"] = f(inp["w_proj_b"][0])
    sh["w_out"] = f(inp["w_out"][0]); sh["w_ff1"] = f(inp["w_ff1"][0]); sh["w_ff2"] = f(inp["w_ff2"][0])
    sh["gmix_pc"] = f(np.asarray(inp["norm_mix"][0]).reshape(8, 128).T)
    sh["gffn_pc"] = f(np.asarray(inp["norm_ffn"][0]).reshape(8, 128).T)
    sh["gfin_bc"] = f(np.broadcast_to(np.asarray(inp["norm_final"])[None, :], (128, D)))
    sh["bgate_pc"] = f(np.asarray(inp["b_gate"][0]).reshape(16, 128).T)
    rb = np.asarray(inp["rel_bias"][0], dtype=np.float32)
    sh["rbpad"] = f(np.concatenate([rb, np.repeat(rb[:, -1:], 128, axis=1)], axis=1))
    sh["rbc"] = f(np.broadcast_to(rb[:, -1][None, :], (128, 8)))
    L2 = lambda a: np.asarray(a).reshape(16, 2, 64).transpose(1, 2, 0).reshape(128, 16)
    sh["are_L"] = f(L2(inp["ssm_a_re"][0])); sh["aim_L"] = f(L2(inp["ssm_a_im"][0]))
    sh["ldt_L"] = f(L2(np.broadcast_to(np.asarray(inp["ssm_log_dt"][0])[:, None], (32, 64))))
    LB = lambda a: np.asarray(a).reshape(16, 2, 64, 16).transpose(1, 2, 0, 3).reshape(128, 16, 16)
    sh["bre_L"] = f(LB(inp["ssm_b_re"][0])); sh["bim_L"] = f(LB(inp["ssm_b_im"][0]))
    LC = lambda a: np.asarray(a).reshape(16, 2, 16, 64).transpose(1, 3, 0, 2).reshape(128, 16, 16)
    sh["cre_L"] = f(LC(inp["ssm_c_re"][0])); sh["cim_L"] = f(LC(inp["ssm_c_im"][0]))
    sh["dcol"] = f(np.tile(np.asarray(inp["ssm_d"][0]).T, (8, 1)))
    sh["ident_bf"] = bf(np.eye(128)); sh["jmat_bf"] = bf(np.eye(128)[::-1])
    sh["ident_f"] = f(np.eye(128))
    rr = np.arange(128)
    sh["tmask"] = f((rr[None, :] // 16) >= (rr[:, None] // 16))
    return sh


_NC_CACHE = {}


def kernel(**inputs):
    xfull = np.asarray(inputs["x"], dtype=np.float32)
    B = xfull.shape[0]
    assert B == NCORES * SEQ_PER_CORE
    shared = host_inputs(inputs)
    if "nc" not in _NC_CACHE:
        _NC_CACHE["nc"] = build_program()
    nc = _NC_CACHE["nc"]
    in_maps = []
    for c in range(NCORES):
        m = dict(shared)
        m["x"] = np.ascontiguousarray(xfull[c * SEQ_PER_CORE:(c + 1) * SEQ_PER_CORE].reshape(SEQ_PER_CORE * SEQ, D))
        in_maps.append(m)
    res = run_bass_kernel_spmd(nc, in_maps, core_ids=list(range(NCORES)))
    outs = [np.asarray(r["out"]).reshape(SEQ_PER_CORE, SEQ, D) for r in res.results]
    return np.concatenate(outs, axis=0).astype(np.float32)
```
